# Optimizing a Trainium2 kernel written in Bass

```python
import math
import jax
import jax.numpy as jnp
from jax import lax
import numpy as np

D_MODEL = 4096
BATCH = 1
SEQ = 8192
DEPTH = 2

MIX_WIDTH = D_MODEL
ATTN_WIDTH = MIX_WIDTH // 2
ATTN_HEAD_DIM = 128
ATTN_HEADS = ATTN_WIDTH // ATTN_HEAD_DIM
DILATION_PATTERNS = ((128, 1), (512, 4), (2048, 16))
ATTN_Q_BLOCK = 64
REL_BUCKETS = 32
REL_MAX_EXACT = REL_BUCKETS // 2
REL_MAX_DISTANCE = 2048
MLSTM_WIDTH = MIX_WIDTH - ATTN_WIDTH
MLSTM_HEADS = 4
MLSTM_V_DIM = MLSTM_WIDTH // MLSTM_HEADS
MLSTM_QK_DIM = MLSTM_V_DIM // 2
MLSTM_QK_WIDTH = MLSTM_HEADS * MLSTM_QK_DIM
MLSTM_CHUNK = 64
CONV_WIDTH = 4
NORM_EPS = 1e-6
SPLIT_SIZES = (ATTN_WIDTH, ATTN_WIDTH, ATTN_WIDTH, ATTN_WIDTH,
               MLSTM_QK_WIDTH, MLSTM_QK_WIDTH, MLSTM_WIDTH, MLSTM_WIDTH, MLSTM_WIDTH,
               MLSTM_HEADS, MLSTM_HEADS)
IN_WIDTH = sum(SPLIT_SIZES)

kernel_name = 'hymba_dilated_attn_mlstm_hybrid'


def rmsnorm(x, g):
    xf = x.astype(jnp.float32)
    y = xf * lax.rsqrt(jnp.mean(xf * xf, axis=-1, keepdims=True) + NORM_EPS)
    return (y * g.astype(jnp.float32)).astype(x.dtype)


def rel_buckets(dist):
    n = dist.astype(np.float32)
    large = REL_MAX_EXACT + np.floor(
        np.log(np.maximum(n, 1.0) / REL_MAX_EXACT)
        / np.log(REL_MAX_DISTANCE / REL_MAX_EXACT) * (REL_BUCKETS - REL_MAX_EXACT))
    large = np.minimum(large, REL_BUCKETS - 1)
    return np.where(n < REL_MAX_EXACT, n, large).astype(np.int32)


def dilated_attention(q, k, v, rel_bias):
    B, S, H, Dh = q.shape
    q = q * (Dh ** -0.5)
    offs = [np.arange(w // d + 1, dtype=np.int32) * d for w, d in DILATION_PATTERNS]
    biases = [rel_bias[rel_buckets(o)].T.astype(jnp.float32) for o in offs]

    def block(bi):
        start = bi * ATTN_Q_BLOCK
        t = start + jnp.arange(ATTN_Q_BLOCK, dtype=jnp.int32)
        qb = lax.dynamic_slice_in_dim(q, start, ATTN_Q_BLOCK, axis=1)
        outs, lses = [], []
        for off, bias in zip(offs, biases):
            idx = t[:, None] - jnp.asarray(off)[None, :]
            valid = idx >= 0
            idx = jnp.maximum(idx, 0)
            kg = k[:, idx]
            vg = v[:, idx]
            s = jnp.einsum('bqhd,bqjhd->bhqj', qb, kg).astype(jnp.float32) + bias[None, :, None, :]
            s = jnp.where(valid[None, None], s, -jnp.inf)
            m = jnp.max(s, axis=-1, keepdims=True)
            p = jnp.exp(s - m)
            l = jnp.sum(p, axis=-1)
            o = jnp.einsum('bhqj,bqjhd->bhqd', p, vg.astype(jnp.float32)) / l[..., None]
            outs.append(o)
            lses.append(m[..., 0] + jnp.log(l))
        wts = jax.nn.softmax(jnp.stack(lses), axis=0)
        o = jnp.einsum('pbhq,pbhqd->bqhd', wts, jnp.stack(outs))
        return o.astype(q.dtype)

    ob = lax.map(block, jnp.arange(S // ATTN_Q_BLOCK, dtype=jnp.int32))
    return ob.transpose(1, 0, 2, 3, 4).reshape(B, S, H, Dh)


def causal_conv(x, w):
    S = x.shape[1]
    xp = jnp.pad(x, ((0, 0), (CONV_WIDTH - 1, 0), (0, 0)))
    return sum(xp[:, j:j + S] * w[j] for j in range(CONV_WIDTH))


def mlstm_chunkwise(q, k, v, ig, fg):
    B, S, H, Dk = q.shape
    Dv = v.shape[-1]
    L = MLSTM_CHUNK
    NC = S // L
    f32 = jnp.float32
    q = q.astype(f32)
    k = k.astype(f32) * (Dk ** -0.5)
    v = v.astype(f32)
    log_f = jax.nn.log_sigmoid(fg.astype(f32))
    log_i = ig.astype(f32)

    def chunks(a):
        return jnp.moveaxis(a.reshape((B, NC, L) + a.shape[2:]), 1, 0).swapaxes(2, 3)

    qc, kc, vc = chunks(q), chunks(k), chunks(v)
    lfc, lic = chunks(log_f), chunks(log_i)
    bc = jnp.cumsum(lfc, axis=-1)
    tri = jnp.tril(jnp.ones((L, L), dtype=bool))

    def step(carry, inp):
        C, n, m = carry
        qx, kx, vx, b, li = inp
        D = b[..., :, None] - b[..., None, :] + li[..., None, :]
        D = jnp.where(tri, D, -jnp.inf)
        m_inter = b + m[..., None]
        m_t = jnp.maximum(m_inter, jnp.max(D, axis=-1))
        w_inter = jnp.exp(m_inter - m_t)
        Sqk = jnp.einsum('bhld,bhsd->bhls', qx, kx) * jnp.exp(D - m_t[..., None])
        num = w_inter[..., None] * jnp.einsum('bhld,bhdv->bhlv', qx, C) + jnp.einsum('bhls,bhsv->bhlv', Sqk, vx)
        nq = w_inter * jnp.einsum('bhld,bhd->bhl', qx, n) + jnp.sum(Sqk, axis=-1)
        h = num / jnp.maximum(jnp.abs(nq), jnp.exp(-m_t))[..., None]
        bL = b[..., -1]
        g = bL[..., None] - b + li
        m_new = jnp.maximum(bL + m, jnp.max(g, axis=-1))
        a = jnp.exp(bL + m - m_new)
        ws = jnp.exp(g - m_new[..., None])
        C_new = a[..., None, None] * C + jnp.einsum('bhs,bhsd,bhsv->bhdv', ws, kx, vx)
        n_new = a[..., None] * n + jnp.einsum('bhs,bhsd->bhd', ws, kx)
        return (C_new, n_new, m_new), h

    init = (jnp.zeros((B, H, Dk, Dv), f32), jnp.zeros((B, H, Dk), f32), jnp.zeros((B, H), f32))
    _, hs = lax.scan(step, init, (qc, kc, vc, bc, lic))
    return hs.transpose(1, 0, 3, 2, 4).reshape(B, S, H, Dv)


def setup_inputs(seed: int = 0) -> dict:
    key = jax.random.key(seed)
    ks = jax.random.split(key, 10)
    f32 = jnp.float32
    x = jax.random.normal(ks[0], (BATCH, SEQ, D_MODEL), f32)
    norm_g = 1.0 + 0.05 * jax.random.normal(ks[1], (DEPTH, D_MODEL), f32)
    w_in = jax.random.normal(ks[2], (DEPTH, D_MODEL, IN_WIDTH), f32) * D_MODEL ** -0.5
    b_i = 0.1 * jax.random.normal(ks[3], (DEPTH, MLSTM_HEADS), f32)
    b_f = jnp.linspace(3.0, 6.0, MLSTM_HEADS, dtype=f32)[None, :] + 0.1 * jax.random.normal(ks[4], (DEPTH, MLSTM_HEADS), f32)
    b_gate = jnp.concatenate([b_i, b_f], axis=-1)
    conv_w = jax.random.normal(ks[5], (DEPTH, CONV_WIDTH, 2 * MLSTM_QK_WIDTH), f32) * CONV_WIDTH ** -0.5
    mlstm_norm_g = 1.0 + 0.05 * jax.random.normal(ks[6], (DEPTH, MLSTM_WIDTH), f32)
    w_out = jax.random.normal(ks[7], (DEPTH, MIX_WIDTH, D_MODEL), f32) * (0.5 * MIX_WIDTH ** -0.5)
    rel_bias = 0.5 * jax.random.normal(ks[8], (REL_BUCKETS, ATTN_HEADS), f32)
    final_g = 1.0 + 0.05 * jax.random.normal(ks[9], (D_MODEL,), f32)
    return {'x': x, 'norm_g': norm_g, 'w_in': w_in, 'b_gate': b_gate, 'conv_w': conv_w,
            'mlstm_norm_g': mlstm_norm_g, 'w_out': w_out, 'rel_bias': rel_bias, 'final_g': final_g}


def reference(x, norm_g, w_in, b_gate, conv_w, mlstm_norm_g, w_out, rel_bias, final_g):
    B, S, _ = x.shape
    split_points = tuple(int(p) for p in np.cumsum(SPLIT_SIZES)[:-1])
    for l in range(DEPTH):
        h = rmsnorm(x, norm_g[l])
        u = jnp.einsum('bsd,de->bse', h, w_in[l])
        qa, ka, va, za, qm, km, vm, om, zm, ig, fg = jnp.split(u, split_points, axis=-1)
        hd = (B, S, ATTN_HEADS, ATTN_HEAD_DIM)
        ya = dilated_attention(qa.reshape(hd), ka.reshape(hd), va.reshape(hd), rel_bias)
        ya = ya.reshape(B, S, ATTN_WIDTH) * jax.nn.silu(za)
        qk = jax.nn.silu(causal_conv(jnp.concatenate([qm, km], axis=-1), conv_w[l]))
        qm, km = jnp.split(qk, 2, axis=-1)
        ig = ig.astype(jnp.float32) + b_gate[l, :MLSTM_HEADS].astype(jnp.float32)
        fg = fg.astype(jnp.float32) + b_gate[l, MLSTM_HEADS:].astype(jnp.float32)
        hm = mlstm_chunkwise(qm.reshape(B, S, MLSTM_HEADS, MLSTM_QK_DIM),
                             km.reshape(B, S, MLSTM_HEADS, MLSTM_QK_DIM),
                             vm.reshape(B, S, MLSTM_HEADS, MLSTM_V_DIM), ig, fg)
        hm = hm * lax.rsqrt(jnp.mean(hm * hm, axis=-1, keepdims=True) + NORM_EPS)
        hm = hm * mlstm_norm_g[l].astype(jnp.float32).reshape(MLSTM_HEADS, MLSTM_V_DIM)
        ym = hm.reshape(B, S, MLSTM_WIDTH).astype(x.dtype) * jax.nn.sigmoid(om) * jax.nn.silu(zm)
        y = jnp.concatenate([ya, ym], axis=-1)
        x = x + jnp.einsum('bse,ed->bsd', y, w_out[l]).astype(x.dtype)
    return rmsnorm(x, final_g)
```

```python
import numpy as np
from contextlib import ExitStack
import concourse.bass as bass
import concourse.mybir as mybir
from concourse.bass_utils import run_bass_kernel_spmd

F32 = mybir.dt.float32
BF16 = mybir.dt.bfloat16
AF = mybir.ActivationFunctionType
ALU = mybir.AluOpType
AX = mybir.AxisListType

NCORES = 8
D = 4096
S = 8192
TOK = S // NCORES
KT = D // 128
DEPTH = 2
EPS = 1e-6
NWC = 2306
PATTERNS = ((128, 1), (512, 4), (2048, 16))
NEG = -30000.0


class Buf:
    def __init__(self, ctx, t, name):
        self.ctx = ctx
        self.t = t
        self.name = name
        self.w = {}
        self.r = {}
        self.dsem = None

    def __getitem__(self, k):
        return self.t[k]

    def ap(self):
        return self.t[:]


def _merge(dst, src):
    for k, v in src.items():
        if dst.get(k, 0) < v:
            dst[k] = v


class Ctx:
    def __init__(self, nc, es):
        self.nc = nc
        self.es = es
        self.eng = {"pe": nc.tensor, "act": nc.scalar, "dve": nc.vector, "pool": nc.gpsimd, "sp": nc.sync}
        self.sems = {}
        self.cnt = {}
        self.waited = {k: {} for k in self.eng}
        for k in self.eng:
            self.sems[k] = es.enter_context(nc.semaphore("e_" + k))
            self.cnt[k] = 0
        self.ges = es
        self.dnext = 0
        self.uid = 0

    def scope(self):
        return _Scope(self)

    def barrier(self):
        for e in self.eng:
            need = {k: v for k, v in self.cnt.items() if v > 0 and k != e}
            self._do_waits(e, need)

    def sbuf(self, name, shape, dt, dma=False):
        self.uid += 1
        name = "%s_%d" % (name, self.uid)
        t = self.es.enter_context(self.nc.sbuf_tensor(name, list(shape), dt))
        b = Buf(self, t, name)
        if dma:
            self.add_dsem(b)
        return b

    def psum(self, name, shape, dt=F32):
        self.uid += 1
        name = "%s_%d" % (name, self.uid)
        t = self.es.enter_context(self.nc.psum_tensor(name, list(shape), dt))
        return Buf(self, t, name)

    def add_dsem(self, b):
        key = "d%d" % self.dnext
        self.dnext += 1
        if key not in self.sems:
            self.sems[key] = self.ges.enter_context(self.nc.semaphore(key))
            self.cnt[key] = 0
        b.dsem = key

    def virt(self, name, dma=False):
        b = Buf(self, None, name)
        if dma:
            self.add_dsem(b)
        return b

    def _do_waits(self, e, need):
        eng = self.eng[e]
        wd = self.waited[e]
        for k, v in need.items():
            if wd.get(k, 0) < v:
                eng.wait_ge(self.sems[k], v)
                wd[k] = v

    def _hazards(self, reads, writes, pwrites):
        need = {}
        for b in reads:
            _merge(need, b.w)
        for b in writes:
            _merge(need, b.w)
            _merge(need, b.r)
        for b in pwrites:
            _merge(need, b.r)
        return need

    def _record(self, ev, reads, writes, pwrites):
        for b in reads:
            _merge(b.r, ev)
        for b in writes:
            b.w = dict(ev)
            b.r = {}
        for b in pwrites:
            _merge(b.w, ev)

    def op(self, e, fn, reads=(), writes=(), pwrites=(), same_eng_ok=False):
        need = self._hazards(reads, writes, pwrites)
        if same_eng_ok:
            need.pop(e, None)
        self._do_waits(e, need)
        ins = fn(self.eng[e])
        self.cnt[e] += 1
        ins.then_inc(self.sems[e], 1)
        self._record({e: self.cnt[e]}, reads, writes, pwrites)

    def dma(self, q, out_ap, in_ap, sem_buf, reads=(), writes=(), pwrites=(), n=1, **kw):
        need = self._hazards(reads, writes, pwrites)
        self._do_waits(q, need)
        key = sem_buf.dsem
        outs = out_ap if isinstance(out_ap, list) else [out_ap]
        ins_ = in_ap if isinstance(in_ap, list) else [in_ap]
        for o, i in zip(outs, ins_):
            self.eng[q].dma_start(out=o, in_=i, **kw).then_inc(self.sems[key], 16)
            self.cnt[key] += 16
        self._record({key: self.cnt[key]}, reads, writes, pwrites)

    def wait_all(self, e, bufs):
        need = {}
        for b in bufs:
            _merge(need, b.w)
            _merge(need, b.r)
        self._do_waits(e, need)


class _Scope:
    def __init__(self, cx):
        self.cx = cx

    def __enter__(self):
        self.old = self.cx.es
        self.cx.es = ExitStack()
        self.cx.dnext = 0
        return self.cx

    def __exit__(self, *a):
        self.cx.barrier()
        self.cx.es.close()
        self.cx.es = self.old
        self.cx.dnext = 0
        return False


def rr(cx, name, n):
    return [cx.virt("%s%d" % (name, i), dma=True) for i in range(n)]


def emit_norm(cx, xT, gn, lidx, hT, final=False):
    with cx.scope():
        ones = cx.sbuf("n_ones", [128, 128], BF16)
        g = cx.sbuf("n_g", [128, 96], F32, dma=True)
        xs = cx.sbuf("n_xs", [128, KT, TOK], F32)
        sq = [cx.sbuf("n_sq%d" % i, [128, TOK], BF16) for i in range(2)]
        ss = cx.psum("n_ss", [128, TOK], F32)
        rstd = cx.sbuf("n_rstd", [128, TOK], F32)
        ho = [cx.sbuf("n_ho%d" % i, [128, TOK], F32 if final else BF16, dma=True) for i in range(3)]
        cx.op("dve", lambda e: e.memset(ones[:], 1.0), writes=[ones])
        cx.dma("sp", g[:], gn[:, :], g, writes=[g])
        xsk = [cx.virt("n_xs%d" % k) for k in range(KT)]
        slots = rr(cx, "n_ld", 4)
        for k in range(KT):
            cx.dma("sp", xs[:, k, :], xT[k * 128:(k + 1) * 128, :], slots[k % 4], writes=[slots[k % 4], xsk[k]])
        for k in range(KT):
            s_ = sq[k % 2]
            cx.op("act", lambda e: e.activation(out=s_[:], in_=xs[:, k, :], func=AF.Square), reads=[xsk[k]], writes=[s_])

            def mm(e):
                e.matmul(ss[:, 0:512], ones[:], s_[:, 0:512], start=(k == 0), stop=(k == KT - 1))
                return e.matmul(ss[:, 512:1024], ones[:], s_[:, 512:1024], start=(k == 0), stop=(k == KT - 1))
            cx.op("pe", mm, reads=[ones, s_], writes=[ss] if k == 0 else [], pwrites=[ss] if k else [])
        cx.op("dve", lambda e: e.tensor_scalar(rstd[:], ss[:], 1.0 / D, EPS, ALU.mult, ALU.add), reads=[ss], writes=[rstd])
        cx.op("act", lambda e: e.activation(out=rstd[:], in_=rstd[:], func=AF.Sqrt), reads=[rstd], writes=[rstd])
        cx.op("dve", lambda e: e.reciprocal(rstd[:], rstd[:]), reads=[rstd], writes=[rstd])
        for k in range(KT):
            h_ = ho[k % 3]
            cx.op("dve", lambda e: e.scalar_tensor_tensor(h_[:], xs[:, k, :], g[:, lidx * 32 + k:lidx * 32 + k + 1], rstd[:], ALU.mult, ALU.mult),
                  reads=[xsk[k], g, rstd], writes=[h_])
            cx.dma("pool", hT[k * 128:(k + 1) * 128, :], h_[:], h_, reads=[h_])


def emit_inproj(cx, hTf, w, uT, uG):
    QS = 128.0 ** -0.5
    for grp, (c0, c1) in enumerate(((0, 1024), (1024, NWC))):
        ncol = c1 - c0
        nct = ncol // 128
        with cx.scope():
            wb = cx.sbuf("a_wb", [128, KT, ncol], BF16)
            st = [cx.sbuf("a_st%d" % i, [128, ncol], F32, dma=True) for i in range(2)]
            hb = [cx.sbuf("a_hb%d" % i, [128, KT, 512], BF16) for i in range(2)]
            hbs = [rr(cx, "a_hbs%d_" % i, 4) for i in range(2)]
            ob = [cx.sbuf("a_ob%d" % i, [128, 512], BF16, dma=True) for i in range(4)]
            og = [cx.sbuf("a_og%d" % i, [2, 512], F32, dma=True) for i in range(2)]
            ps = [cx.psum("a_ps%d" % i, [128, 512], F32) for i in range(6)]
            wbv = cx.virt("a_wbv")
            for k in range(KT):
                s_ = st[k % 2]
                cx.dma("sp", s_[:], w[k * 128:(k + 1) * 128, c0:c1], s_, writes=[s_])
                en = ("act", "dve", "pool")[k % 3]
                if en == "act":
                    cx.op(en, lambda e: e.copy(wb[:, k, :], s_[:]), reads=[s_], pwrites=[wbv])
                else:
                    cx.op(en, lambda e: e.tensor_copy(wb[:, k, :], s_[:]), reads=[s_], pwrites=[wbv])
            it = 0
            for tb in range(S // 512):
                off = tb * 512
                h_ = hb[tb % 2]
                hv = hbs[tb % 2]
                src = hTf.rearrange("(k p) t -> p k t", p=128)
                for q in range(4):
                    cx.dma("sp", h_[:, q * 8:(q + 1) * 8, :], src[:, q * 8:(q + 1) * 8, off:off + 512], hv[q], writes=[hv[q]])
                for ct in range(nct):
                    p_ = ps[it % 6]

                    def mm(e):
                        for k in range(KT):
                            ins = e.matmul(p_[:], wb[:, k, ct * 128:(ct + 1) * 128], h_[:, k, :], start=(k == 0), stop=(k == KT - 1))
                        return ins
                    cx.op("pe", mm, reads=[wbv] + hv, writes=[p_])
                    o_ = ob[it % 4]
                    scale = QS if (grp == 0 and ct < 2) else 1.0
                    if it % 2 == 0:
                        cx.op("act", lambda e: e.activation(out=o_[:], in_=p_[:], func=AF.Copy, scale=scale), reads=[p_], writes=[o_])
                    else:
                        cx.op("dve", lambda e: e.tensor_scalar(o_[:], p_[:], scale, None, ALU.mult), reads=[p_], writes=[o_])
                    gct = c0 // 128 + ct
                    cx.dma("pool", uT[gct * 128:(gct + 1) * 128, tb * 512:(tb + 1) * 512], o_[:], o_, reads=[o_])
                    it += 1
                if grp == 1:
                    p_ = ps[it % 6]

                    def mmg(e):
                        for k in range(KT):
                            ins = e.matmul(p_[0:2, :], wb[:, k, ncol - 2:ncol], h_[:, k, :], start=(k == 0), stop=(k == KT - 1))
                        return ins
                    cx.op("pe", mmg, reads=[wbv] + hv, writes=[p_])
                    o_ = og[tb % 2]
                    cx.op("dve", lambda e: e.tensor_copy(o_[:], p_[0:2, :]), reads=[p_], writes=[o_])
                    cx.dma("pool", uG[:, tb * 512:(tb + 1) * 512], o_[:], o_, reads=[o_])
                    it += 1


def core_cols(c):
    hd, half = c // 2, c % 2
    a = np.arange(256)
    cols = []
    for g in range(4):
        cols.append(g * 2048 + 256 * c + a)
    base = 8192
    cols.append(base + 256 * hd + a)
    cols.append(base + 1024 + 256 * hd + a)
    base2 = 8192 + 2048
    for g in range(3):
        cols.append(base2 + g * 2048 + 512 * hd + 256 * half + a)
    cols.append(np.array([16384 + hd]))
    cols.append(np.array([16388 + hd]))
    return np.concatenate(cols)


def ycol_order():
    idx = []
    for c in range(NCORES):
        idx.append(256 * c + np.arange(256))
        idx.append(2048 + 256 * c + np.arange(256))
    return np.concatenate(idx)


def rel_buckets_np(dist):
    n = dist.astype(np.float32)
    large = 16 + np.floor(np.log(np.maximum(n, 1.0) / 16) / np.log(2048 / 16) * 16)
    large = np.minimum(large, 31)
    return np.where(n < 16, n, large).astype(np.int32)


def bias_tables(rel_bias, c):
    rb = np.concatenate([rel_bias, np.full((1, rel_bias.shape[1]), NEG, np.float32)], axis=0)
    i = np.arange(128)[:, None]
    j = np.arange(128)[None, :]
    out = np.zeros((128, 2, 3, 256), np.float32)
    for pi, (wd, d) in enumerate(PATTERNS):
        o_prev = j - i + 128
        o_diag = j - i
        b_prev = np.where(o_prev <= 128, rel_buckets_np(np.maximum(o_prev, 0) * d), 32)
        b_diag = np.where(o_diag >= 0, rel_buckets_np(np.maximum(o_diag, 0) * d), 32)
        for a in range(2):
            out[:, a, pi, 0:128] = rb[b_prev, 2 * c + a]
            out[:, a, pi, 128:256] = rb[b_diag, 2 * c + a]
    return out


def prep_common(norm_g, final_g, b_gate, conv_w, mlstm_norm_g, rel_bias):
    gn = np.zeros((128, 96), np.float32)
    for l in range(DEPTH):
        gn[:, l * 32:(l + 1) * 32] = norm_g[l].reshape(32, 128).T
    gn[:, 64:96] = final_g.reshape(32, 128).T
    per_core = []
    tri = np.triu(np.ones((64, 64), np.float32))
    for c in range(NCORES):
        hd, half = c // 2, c % 2
        cw = np.zeros((128, DEPTH, 4, 4), np.float32)
        bg = np.zeros((128, DEPTH, 2), np.float32)
        mg = np.zeros((128, DEPTH, 2), np.float32)
        for l in range(DEPTH):
            for ti in range(4):
                base = (hd * 256 + ti * 128) if ti < 2 else (1024 + hd * 256 + (ti - 2) * 128)
                cw[:, l, ti, :] = conv_w[l][:, base:base + 128].T
            bg[:, l, 0] = b_gate[l, hd]
            bg[:, l, 1] = b_gate[l, 4 + hd]
            for j in range(2):
                b0 = hd * 512 + half * 256 + j * 128
                mg[:, l, j] = mlstm_norm_g[l, b0:b0 + 128]
        per_core.append(dict(cw=cw.reshape(128, -1), bg=bg.reshape(128, -1), mg=mg.reshape(128, -1),
                             bias=bias_tables(rel_bias, c).reshape(128, -1)))
    consts = dict(gn=gn, ident=np.eye(128, dtype=np.float32), tri=tri)
    return consts, per_core


def emit_attention(cx, uT, biasD, identD, yT, pats=(0, 1, 2), heads=(0, 1)):
    with cx.scope():
        identf = cx.sbuf("b_idf", [128, 128], F32, dma=True)
        ident = cx.sbuf("b_id", [128, 128], BF16)
        ones = cx.sbuf("b_ones", [128, 128], BF16)
        biasf = cx.sbuf("b_bf", [128, 2 * 3 * 256], F32, dma=True)
        biasb = cx.sbuf("b_bb", [128, 2 * 3 * 256], BF16)
        cx.dma("sp", identf[:], identD[:, :], identf, writes=[identf])
        cx.dma("sp", biasf[:], biasD[:, :], biasf, writes=[biasf])
        cx.op("dve", lambda e: e.tensor_copy(ident[:], identf[:]), reads=[identf], writes=[ident])
        cx.op("dve", lambda e: e.tensor_copy(biasb[:], biasf[:]), reads=[biasf], writes=[biasb])
        cx.op("dve", lambda e: e.memset(ones[:], 1.0), writes=[ones])
        qT, kT, vT, zT = [cx.sbuf("b_" + n, [128, S], BF16, dma=True) for n in "qkvz"]
        acc = cx.sbuf("b_acc", [128, 2, S], F32)
        accb = cx.virt("b_accv")
        vtok = [cx.sbuf("b_vt%d" % i, [128, 128], BF16) for i in range(3)]
        pT = [cx.sbuf("b_pT%d" % i, [128, 256], BF16) for i in range(2)]
        sps = [cx.psum("b_sp%d" % i, [128, 256], F32) for i in range(2)]
        nds = [cx.psum("b_nd%d" % i, [128, 2, 128], F32) for i in range(2)]
        tps = [cx.psum("b_tp%d" % i, [128, 128], BF16) for i in range(2)]
        rec = cx.sbuf("b_rec", [128, 2048], F32)
        sz = cx.sbuf("b_sz", [128, 2048], F32)
        yo = [cx.sbuf("b_yo%d" % i, [128, 2048], BF16, dma=True) for i in range(2)]
        it = 0
        pending = None
        for a in heads:
            for n_, (t_, r0) in enumerate(((qT, 0), (kT, 256), (vT, 512), (zT, 768))):
                cx.dma("sp", t_[:], uT[r0 + a * 128:r0 + (a + 1) * 128, :], t_, writes=[t_])
            for pi, (wd, d) in enumerate(PATTERNS):
                if pi not in pats:
                    continue
                nqb = S // (128 * d)
                bofs = (a * 3 + pi) * 256
                for r in range(d):
                    def sl(b):
                        st_ = r + d * 128 * b
                        return slice(st_, st_ + d * 127 + 1, d)
                    for qb in range(nqb):
                        tp_, vt_ = tps[it % 2], vtok[it % 3]
                        cx.op("pe", lambda e: e.transpose(tp_[:], vT[:, sl(qb)], ident[:]), reads=[vT, ident], writes=[tp_])
                        cx.op("act", lambda e: e.copy(vt_[:], tp_[:]), reads=[tp_], writes=[vt_])
                        sp_, p_, nd_ = sps[it % 2], pT[it % 2], nds[it % 2]
                        c0 = 0 if qb > 0 else 128

                        def mm1(e, sp_=sp_, qb=qb, c0=c0, sl=sl):
                            e.matmul(sp_[:, 128:256], kT[:, sl(qb)], qT[:, sl(qb)], start=True, stop=False)
                            if qb > 0:
                                e.matmul(sp_[:, 0:128], kT[:, sl(qb - 1)], qT[:, sl(qb)], start=False, stop=False)
                            return e.matmul(sp_[:, c0:256], ident[:], biasb[:, bofs + c0:bofs + 256], start=False, stop=True)
                        cx.op("pe", mm1, reads=[kT, qT, ident, biasb], writes=[sp_])
                        cx.op("act", lambda e: e.activation(out=p_[:, c0:256], in_=sp_[:, c0:256], func=AF.Exp), reads=[sp_], writes=[p_])
                        if pending is not None:
                            pending()
                        vprev = vtok[(it - 1) % 3]
                        asl = acc[:, :, sl(qb)]

                        def stage2(qb=qb, p_=p_, nd_=nd_, vt_=vt_, vprev=vprev, asl=asl, pi=pi):
                            def mm2(e):
                                if qb > 0:
                                    e.matmul(nd_[:, 0, :], vprev[:], p_[:, 0:128], start=True, stop=False)
                                e.matmul(nd_[:, 0, :], vt_[:], p_[:, 128:256], start=(qb == 0), stop=True)
                                if qb > 0:
                                    e.matmul(nd_[:, 1, :], ones[:], p_[:, 0:128], start=True, stop=False)
                                return e.matmul(nd_[:, 1, :], ones[:], p_[:, 128:256], start=(qb == 0), stop=True)
                            cx.op("pe", mm2, reads=[p_, vt_, vprev, ones], writes=[nd_])
                            if pi == pats[0]:
                                cx.op("dve", lambda e: e.tensor_copy(asl, nd_[:]), reads=[nd_], pwrites=[accb])
                            else:
                                cx.op("dve", lambda e: e.tensor_tensor(asl, nd_[:], asl, ALU.add), reads=[nd_], pwrites=[accb])
                        pending = stage2
                        it += 1
            if pending is not None:
                pending()
                pending = None
            for ch in range(S // 2048):
                cs_ = slice(ch * 2048, (ch + 1) * 2048)
                y_ = yo[ch % 2]
                cx.op("dve", lambda e: e.reciprocal(rec[:], acc[:, 1, cs_]), reads=[accb], writes=[rec])
                cx.op("dve", lambda e: e.tensor_tensor(rec[:], acc[:, 0, cs_], rec[:], ALU.mult), reads=[accb, rec], writes=[rec])
                cx.op("act", lambda e: e.activation(out=sz[:], in_=zT[:, cs_], func=AF.Silu), reads=[zT], writes=[sz])
                cx.op("dve", lambda e: e.tensor_tensor(y_[:], rec[:], sz[:], ALU.mult), reads=[rec, sz], writes=[y_])
                cx.dma("pool", yT[a * 128:(a + 1) * 128, cs_], y_[:], y_, reads=[y_])


def emit_mlstm(cx, uT, uG, cwD, bgD, mgD, identD, triD, l, yT, ssm, gsc):
    L = 64
    NCH = S // L
    LN16 = float(np.log(16.0))
    with cx.scope():
        identf = cx.sbuf("m_idf", [128, 128], F32, dma=True)
        ident = cx.sbuf("m_id", [128, 128], BF16)
        onesf = cx.sbuf("m_1f", [128, 128], F32)
        ones = cx.sbuf("m_1b", [128, 128], BF16)
        tri = cx.sbuf("m_tri", [64, 64], F32, dma=True)
        cw = cx.sbuf("m_cw", [128, 32], F32, dma=True)
        bg = cx.sbuf("m_bg", [128, 4], F32, dma=True)
        mg = cx.sbuf("m_mg", [128, 4], F32, dma=True)
        for t_, d_ in ((identf, identD), (tri, triD), (cw, cwD), (bg, bgD), (mg, mgD)):
            cx.dma("sp", t_[:], d_[:, :], t_, writes=[t_])
        cx.op("dve", lambda e: e.tensor_copy(ident[:], identf[:]), reads=[identf], writes=[ident])
        cx.op("dve", lambda e: e.memset(onesf[:], 1.0), writes=[onesf])
        cx.op("dve", lambda e: e.memset(ones[:], 1.0), writes=[ones])
        a_bc = cx.sbuf("m_abc", [128, 128], F32)
        wT = cx.sbuf("m_wT", [64, 128], F32)
        gscv = cx.virt("m_gscv")
        with cx.scope():
            ig = cx.sbuf("g_ig", [128, L], F32, dma=True)
            fg = cx.sbuf("g_fg", [128, L], F32, dma=True)
            cx.dma("sp", ig[:], uG[0:1, :].rearrange("o (c s) -> (o c) s", s=L), ig, writes=[ig])
            cx.dma("sp", fg[:], uG[1:2, :].rearrange("o (c s) -> (o c) s", s=L), fg, writes=[fg])
            li = cx.sbuf("g_li", [128, L], F32)
            nbf = cx.sbuf("g_nbf", [128, 1], F32)
            e1 = cx.sbuf("g_e1", [128, L], F32)
            cs = cx.sbuf("g_cs", [128, L], F32)
            dd = cx.sbuf("g_dd", [128, L], F32)
            cols2 = cx.sbuf("g_c2", [128, 2], F32)
            rows = cx.sbuf("g_rows", [1, 640], F32)
            psr = cx.psum("g_psr", [128, 256], F32)
            Mcol = cx.sbuf("g_M", [128, 2], F32)
            Wcs = cx.sbuf("g_W", [128, L], F32)
            Rcl = cx.sbuf("g_R", [128, L], F32, dma=True)
            cx.op("dve", lambda e: e.tensor_scalar(li[:], ig[:], bg[:, 2 * l:2 * l + 1], None, ALU.add), reads=[ig, bg], writes=[li])
            cx.op("dve", lambda e: e.tensor_scalar(nbf[:], bg[:, 2 * l + 1:2 * l + 2], -1.0, None, ALU.mult), reads=[bg], writes=[nbf])
            cx.op("act", lambda e: e.activation(out=e1[:], in_=fg[:], func=AF.Exp, bias=nbf[:, 0:1], scale=-1.0), reads=[fg, nbf], writes=[e1])
            cx.op("dve", lambda e: e.tensor_scalar(e1[:], e1[:], 1.0, None, ALU.add), reads=[e1], writes=[e1])
            cx.op("act", lambda e: e.activation(out=e1[:], in_=e1[:], func=AF.Ln), reads=[e1], writes=[e1])
            cx.op("dve", lambda e: e.tensor_tensor_scan(cs[:], onesf[:, 0:L], e1[:], 0.0, ALU.mult, ALU.add), reads=[e1, onesf], writes=[cs])
            cx.op("dve", lambda e: e.tensor_tensor(dd[:], li[:], cs[:], ALU.add), reads=[li, cs], writes=[dd])
            cx.op("dve", lambda e: e.reduce_max(cols2[:, 0:1], dd[:], AX.X), reads=[dd], writes=[cols2])
            cx.op("dve", lambda e: e.tensor_scalar(cols2[:, 1:2], cs[:, L - 1:L], -1.0, None, ALU.mult), reads=[cs, cols2], writes=[cols2])

            def mmr(e):
                e.matmul(psr[0:1, 0:128], cols2[:, 0:1], identf[:], start=True, stop=True)
                return e.matmul(psr[0:1, 128:256], cols2[:, 1:2], identf[:], start=False, stop=True)
            cx.op("pe", mmr, reads=[cols2, identf], writes=[psr])
            cx.op("dve", lambda e: e.tensor_copy(rows[:, 0:256], psr[0:1, 0:256]), reads=[psr], writes=[rows])
            cx.op("dve", lambda e: e.tensor_tensor(rows[:, 256:384], rows[:, 0:128], rows[:, 128:256], ALU.add), reads=[rows], writes=[rows])
            cx.op("dve", lambda e: e.tensor_tensor_scan(rows[:, 384:512], rows[:, 128:256], rows[:, 256:384], 0.0, ALU.add, ALU.max), reads=[rows], writes=[rows])
            cx.op("dve", lambda e: e.memset(rows[:, 512:513], 0.0), reads=[rows], writes=[rows])
            cx.op("dve", lambda e: e.tensor_copy(rows[:, 513:640], rows[:, 384:511]), reads=[rows], writes=[rows])
            cx.op("dve", lambda e: e.tensor_tensor(rows[:, 0:128], rows[:, 512:640], rows[:, 0:128], ALU.max), reads=[rows], writes=[rows])
            cx.op("dve", lambda e: e.tensor_tensor(rows[:, 128:256], rows[:, 512:640], rows[:, 0:128], ALU.subtract), reads=[rows], writes=[rows])
            cx.op("act", lambda e: e.activation(out=rows[:, 128:256], in_=rows[:, 128:256], func=AF.Exp), reads=[rows], writes=[rows])

            def mmc(e):
                e.matmul(psr[:, 0:1], rows[0:1, 0:128], onesf[0:1, 0:1], start=True, stop=True)
                return e.matmul(psr[:, 128:256], onesf[0:1, :], rows[0:1, 128:256], start=False, stop=True)
            cx.op("pe", mmc, reads=[rows, onesf], writes=[psr])
            cx.op("dve", lambda e: e.tensor_copy(Mcol[:, 0:1], psr[:, 0:1]), reads=[psr], writes=[Mcol])
            cx.op("dve", lambda e: e.tensor_copy(a_bc[:], psr[:, 128:256]), reads=[psr], writes=[a_bc])
            cx.op("dve", lambda e: e.tensor_scalar(Mcol[:, 1:2], Mcol[:, 0:1], -1.0, -LN16, ALU.mult, ALU.add), reads=[Mcol], writes=[Mcol])
            cx.op("act", lambda e: e.activation(out=Wcs[:], in_=dd[:], func=AF.Exp, bias=Mcol[:, 1:2], scale=1.0), reads=[dd, Mcol], writes=[Wcs])
            cx.op("act", lambda e: e.activation(out=Rcl[:], in_=cs[:], func=AF.Exp, bias=Mcol[:, 0:1], scale=-1.0), reads=[cs, Mcol], writes=[Rcl])
            cx.op("pe", lambda e: e.matmul(psr[0:64, 0:128], Wcs[:], identf[:], start=True, stop=True), reads=[Wcs, identf, a_bc, Mcol], writes=[psr])
            cx.op("dve", lambda e: e.tensor_copy(wT[:], psr[0:64, 0:128]), reads=[psr], writes=[wT])
            cx.dma("sp", gsc[0:1, :].rearrange("o (c s) -> (o c) s", s=L), Rcl[:], Rcl, reads=[Rcl], writes=[gscv])
        qk = cx.sbuf("m_qk", [128, 4, S], BF16)
        qkv = cx.virt("m_qkv")
        with cx.scope():
            dg = cx.sbuf("c_dg", [128, 16, 128], BF16)
            for ti in range(4):
                for j in range(4):
                    col = l * 16 + ti * 4 + j
                    cx.op("dve", lambda e: e.tensor_scalar(dg[:, ti * 4 + j, :], identf[:], cw[:, col:col + 1], None, ALU.mult), reads=[identf, cw], pwrites=[dg])
            pre = [cx.sbuf("c_pre%d" % i, [128, S + 3], BF16, dma=True) for i in range(2)]
            cps = [cx.psum("c_ps%d" % i, [128, 512], F32) for i in range(4)]
            it = 0
            for ti in range(4):
                r0 = 1024 + ti * 128
                p_ = pre[ti % 2]
                cx.op("pool", lambda e: e.memset(p_[:, 0:3], 0.0), writes=[p_])
                cx.dma("sp", p_[:, 3:S + 3], uT[r0:r0 + 128, :], p_, reads=[p_], pwrites=[p_])
                for tb in range(S // 512):
                    t0 = tb * 512
                    c_ = cps[it % 4]

                    def mmc(e):
                        for j in range(4):
                            ins = e.matmul(c_[:], dg[:, ti * 4 + j, :], p_[:, t0 + j:t0 + j + 512], start=(j == 0), stop=(j == 3))
                        return ins
                    cx.op("pe", mmc, reads=[dg, p_], writes=[c_])
                    cx.op("act", lambda e: e.activation(out=qk[:, ti, t0:t0 + 512], in_=c_[:], func=AF.Silu), reads=[c_], pwrites=[qkv])
                    it += 1
        vT = cx.sbuf("m_vT", [128, 2, S], BF16, dma=True)
        cx.dma("sp", vT[:], uT[1536:1792, :].rearrange("(j p) t -> p j t", p=128), vT, writes=[vT])
        Sf = cx.sbuf("m_Sf", [128, 2, 384], F32)
        Sb = [cx.sbuf("m_Sb%d" % i, [128, 2, 384], BF16) for i in range(2)]
        kw = [cx.sbuf("m_kw%d" % i, [64, 256], BF16) for i in range(2)]
        vext = [cx.sbuf("m_vx%d" % i, [64, 384], BF16) for i in range(2)]
        sq = [cx.sbuf("m_sq%d" % i, [64, 64], BF16) for i in range(2)]
        qa = [cx.sbuf("m_qa%d" % i, [128, 2, 64], BF16) for i in range(2)]
        xyg = [cx.sbuf("m_xy%d" % i, [128, 3, 512], F32) for i in range(2)]
        tppT = cx.psum("m_tp", [64, 2, 4, 128], BF16)
        tpp = [Buf(cx, tppT.t[:, i], "m_tp%d" % i) for i in range(2)]
        sppT = cx.psum("m_sp", [64, 2, 64], F32)
        spp = [Buf(cx, sppT.t[:, i, :], "m_sp%d" % i) for i in range(2)]
        o3T = cx.psum("m_o3", [128, 2, 3, 64], F32)
        o3 = [Buf(cx, o3T.t[:, i], "m_o3%d" % i) for i in range(2)]
        U = [cx.psum("m_U%d" % i, [128, 2, 512], F32) for i in range(2)]
        ssp = cx.psum("m_ssp", [128, 512], F32)
        rb = [cx.sbuf("f_rb%d" % i, [128, 512], F32, dma=True) for i in range(2)]
        oz = [cx.sbuf("f_oz%d" % i, [128, 4, 512], BF16, dma=True) for i in range(2)]
        t1 = cx.sbuf("f_t1", [128, 512], F32)
        hT = cx.sbuf("f_hT", [128, 2, 512], F32)
        sqh = cx.sbuf("f_sqh", [128, 2, 512], BF16)
        ssr = [cx.sbuf("f_ssr%d" % i, [1, 512], F32, dma=True) for i in range(2)]
        sg = cx.sbuf("f_sg", [128, 4, 512], F32)
        tz = cx.sbuf("f_tz", [128, 2, 512], F32)
        yo = [cx.sbuf("f_yo%d" % i, [128, 2, 512], BF16, dma=True) for i in range(2)]
        cx.op("dve", lambda e: e.memset(Sf[:], 0.0), writes=[Sf])
        cx.op("dve", lambda e: e.memset(Sb[0][:], 0.0), writes=[Sb[0]])
        for i in range(2):
            cx.op("pool", lambda e: e.memset(vext[i][:], 1.0), writes=[vext[i]])
        for c in range(NCH):
            tok = slice(c * L, (c + 1) * L)
            i2 = c % 2
            tp_, kw_, vx_, sp_, sq_, qa_, o3_ = tpp[i2], kw[i2], vext[i2], spp[i2], sq[i2], qa[i2], o3[i2]
            Sc, Sn = Sb[i2], Sb[1 - i2]
            g = c // 8
            xy_ = xyg[g % 2]

            def tr(e):
                for i in range(4):
                    src = qk[:, 2 + i, tok] if i < 2 else vT[:, i - 2, tok]
                    ins = e.transpose(tp_[:, i, :], src, ident[:])
                return ins
            cx.op("pe", tr, reads=[qkv, vT, ident], writes=[tp_])
            cx.op("act", lambda e: e.activation(out=kw_[:].rearrange("p (a b) -> p a b", a=2), in_=tp_[:, 0:2, :], func=AF.Copy, scale=wT[:, c:c + 1]),
                  reads=[tp_, wT], writes=[kw_])
            cx.op("dve", lambda e: e.tensor_copy(vx_[:, 0:256].rearrange("p (a b) -> p a b", a=2), tp_[:, 2:4, :]), reads=[tp_], pwrites=[vx_])

            def mms(e):
                e.matmul(sp_[:], qk[:, 2, tok], qk[:, 0, tok], start=True, stop=False)
                return e.matmul(sp_[:], qk[:, 3, tok], qk[:, 1, tok], start=False, stop=True)
            cx.op("pe", mms, reads=[qkv], writes=[sp_])
            cx.op("dve", lambda e: e.scalar_tensor_tensor(sq_[:], sp_[:], wT[:, c:c + 1], tri[:], ALU.mult, ALU.mult), reads=[sp_, wT, tri], writes=[sq_])
            cx.op("pool", lambda e: e.tensor_scalar(qa_[:], qk[:, 0:2, tok], a_bc[:, c:c + 1], None, ALU.mult), reads=[qkv, a_bc], writes=[qa_])

            U_ = U[i2]

            def mmu(e):
                e.matmul(U_[:, 0, 0:384], kw_[:, 0:128], vx_[:], start=True, stop=True)
                return e.matmul(U_[:, 1, 0:384], kw_[:, 128:256], vx_[:], start=True, stop=True)
            cx.op("pe", mmu, reads=[kw_, vx_], writes=[U_])

            def mmo(e):
                for j3 in range(3):
                    cs_ = slice(j3 * 128, (j3 + 1) * 128)
                    e.matmul(o3_[:, j3, :], Sc[:, 0, cs_], qa_[:, 0, :], start=(j3 == 0), stop=False)
                    e.matmul(o3_[:, j3, :], Sc[:, 1, cs_], qa_[:, 1, :], start=False, stop=False)
                    ins = e.matmul(o3_[:, j3, :], vx_[:, cs_], sq_[:], start=False, stop=True)
                return ins
            cx.op("pe", mmo, reads=[Sc, qa_, vx_, sq_], writes=[o3_])
            cx.op("act", lambda e: e.copy(xy_[:, :, (c % 8) * L:(c % 8 + 1) * L], o3_[:]), reads=[o3_], pwrites=[xy_])
            cx.op("dve", lambda e: e.scalar_tensor_tensor(Sf[:], Sf[:], a_bc[:, c:c + 1], U_[:, :, 0:384], ALU.mult, ALU.add),
                  reads=[Sf, a_bc, U_], writes=[Sf])
            cx.op("act", lambda e: e.copy(Sn[:], Sf[:]), reads=[Sf], writes=[Sn])
            if c % 8 == 7:
                gs = slice(g * 512, (g + 1) * 512)
                rb_, oz_, ssr_, yo_ = rb[g % 2], oz[g % 2], ssr[g % 2], yo[g % 2]
                cx.dma("sp", rb_[:], gsc[0:1, gs].partition_broadcast(128), rb_, reads=[gscv], writes=[rb_])
                cx.dma("sp", oz_[:], uT[1792:2304, gs].rearrange("(i p) t -> p i t", p=128), oz_, writes=[oz_])
                cx.op("dve", lambda e: e.tensor_tensor(t1[:], rb_[:], xy_[:, 2, :], ALU.mult), reads=[rb_, xy_], writes=[t1])
                cx.op("act", lambda e: e.activation(out=t1[:], in_=t1[:], func=AF.Abs), reads=[t1], writes=[t1])
                cx.op("dve", lambda e: e.tensor_scalar(t1[:], t1[:], 1.0, None, ALU.max), reads=[t1], writes=[t1])
                cx.op("dve", lambda e: e.reciprocal(t1[:], t1[:]), reads=[t1], writes=[t1])
                cx.op("dve", lambda e: e.tensor_tensor(t1[:], t1[:], rb_[:], ALU.mult), reads=[t1, rb_], writes=[t1])
                for j in range(2):
                    cx.op("dve", lambda e: e.tensor_tensor(hT[:, j, :], xy_[:, j, :], t1[:], ALU.mult), reads=[xy_, t1], writes=[hT] if j == 0 else [], pwrites=[hT] if j else [])
                cx.op("act", lambda e: e.activation(out=sqh[:], in_=hT[:], func=AF.Square), reads=[hT], writes=[sqh])

                def mmss(e):
                    e.matmul(ssp[:], ones[:], sqh[:, 0, :], start=True, stop=False)
                    return e.matmul(ssp[:], ones[:], sqh[:, 1, :], start=False, stop=True)
                cx.op("pe", mmss, reads=[ones, sqh], writes=[ssp])
                cx.op("dve", lambda e: e.tensor_copy(ssr_[:], ssp[0:1, :]), reads=[ssp], writes=[ssr_])
                cx.dma("pool", ssm[0:1, gs], ssr_[:], ssr_, reads=[ssr_])
                cx.op("act", lambda e: e.activation(out=sg[:], in_=oz_[:], func=AF.Sigmoid), reads=[oz_], writes=[sg])
                cx.op("pool", lambda e: e.tensor_tensor(tz[:], oz_[:, 2:4, :], sg[:, 2:4, :], ALU.mult), reads=[oz_, sg], writes=[tz])
                for j in range(2):
                    cx.op("dve", lambda e: e.scalar_tensor_tensor(hT[:, j, :], hT[:, j, :], mg[:, 2 * l + j:2 * l + j + 1], sg[:, j, :], ALU.mult, ALU.mult),
                          reads=[hT, mg, sg], writes=[hT])
                cx.op("dve", lambda e: e.tensor_tensor(yo_[:], hT[:], tz[:], ALU.mult), reads=[hT, tz], writes=[yo_])
                cx.dma("pool", yT[256:512, gs].rearrange("(j p) t -> p j t", p=128), yo_[:], yo_, reads=[yo_])


def emit_pass1(cx, xin, xout, sspD, op=None):
    NB = S // 512
    with cx.scope():
        ones = cx.sbuf("p_ones", [128, 128], BF16)
        cx.op("dve", lambda e: e.memset(ones[:], 1.0), writes=[ones])
        xb = [cx.sbuf("p_xb%d" % i, [128, 4, 512], F32, dma=True) for i in range(2)]
        sq = [cx.sbuf("p_sq%d" % i, [128, 4, 512], BF16) for i in range(2)]
        ssr = [cx.sbuf("p_ssr%d" % i, [1, 512], F32, dma=True) for i in range(2)]
        ssps = cx.psum("p_ssps", [128, 512], F32)
        if op is not None:
            wob = cx.sbuf("p_wob", [128, KT, 512], BF16)
            wst = [cx.sbuf("p_wst%d" % i, [128, 512], F32, dma=True) for i in range(2)]
            wobv = cx.virt("p_wobv")
            sel = cx.sbuf("p_sel", [8, 4 * 128], F32, dma=True)
            cx.dma("sp", sel[:], op["selD"][:, :], sel, writes=[sel])
            for k in range(KT):
                s_ = wst[k % 2]
                cx.dma("sp", s_[:], op["wo"][k * 128:(k + 1) * 128, :], s_, writes=[s_])
                if k % 2 == 0:
                    cx.op("dve", lambda e: e.tensor_copy(wob[:, k, :], s_[:]), reads=[s_], pwrites=[wobv])
                else:
                    cx.op("pool", lambda e: e.tensor_copy(wob[:, k, :], s_[:]), reads=[s_], pwrites=[wobv])
            yb = [cx.sbuf("p_yb%d" % i, [128, KT, 512], BF16) for i in range(2)]
            ybs = [rr(cx, "p_ybs%d_" % i, 4) for i in range(2)]
            ybv = [cx.virt("p_ybv%d" % i) for i in range(2)]
            ssmb = [cx.sbuf("p_ssmb%d" % i, [8, 512], F32, dma=True) for i in range(2)]
            rs = [cx.sbuf("p_rs%d" % i, [128, 512], F32) for i in range(2)]
            rsp = [cx.psum("p_rsp%d" % i, [128, 512], F32) for i in range(2)]
            ps = [cx.psum("p_ps%d" % i, [128, 512], F32) for i in range(3)]
            x2 = [cx.sbuf("p_x2%d" % i, [128, 4, 512], F32, dma=True) for i in range(2)]
        it = 0
        for tb in range(NB):
            ts_ = slice(tb * 512, (tb + 1) * 512)
            xb_, sq_, ssr_ = xb[tb % 2], sq[tb % 2], ssr[tb % 2]
            cx.dma("sp", xb_[:], xin[:, ts_].rearrange("(j p) t -> p j t", p=128), xb_, writes=[xb_])
            if op is not None:
                yb_, ys_, yv_, sm_, x2_ = yb[tb % 2], ybs[tb % 2], ybv[tb % 2], ssmb[tb % 2], x2[tb % 2]
                ysrc = op["yTf"].rearrange("(k p) t -> p k t", p=128)
                for q in range(4):
                    cx.dma("sp" if q % 2 == 0 else "act", yb_[:, q * 8:(q + 1) * 8, :], ysrc[:, q * 8:(q + 1) * 8, ts_], ys_[q], writes=[ys_[q]], reads=[yv_])
                cx.dma("sp", sm_[:], op["ssmf"][:, ts_], sm_, writes=[sm_])
                for hd in range(4):
                    rp_, rs_ = rsp[hd % 2], rs[hd % 2]
                    cx.op("pe", lambda e: e.matmul(rp_[:], sel[:, hd * 128:(hd + 1) * 128], sm_[:], start=True, stop=True), reads=[sel, sm_], writes=[rp_])
                    cx.op("dve", lambda e: e.tensor_scalar(rs_[:], rp_[:], 1.0 / 512, EPS, ALU.mult, ALU.add), reads=[rp_], writes=[rs_])
                    cx.op("act", lambda e: e.activation(out=rs_[:], in_=rs_[:], func=AF.Sqrt), reads=[rs_], writes=[rs_])
                    cx.op("dve", lambda e: e.reciprocal(rs_[:], rs_[:]), reads=[rs_], writes=[rs_])
                    n_ = 0
                    for r in (2 * hd, 2 * hd + 1):
                        for j in (2, 3):
                            kt = r * 4 + j
                            en = "pool" if n_ % 2 else "dve"
                            cx.op(en, lambda e: e.tensor_tensor(yb_[:, kt, :], yb_[:, kt, :], rs_[:], ALU.mult), reads=[rs_] + ys_, pwrites=[yv_])
                            n_ += 1
                for dt in range(4):
                    p_ = ps[it % 3]

                    def mm(e):
                        for k in range(KT):
                            ins = e.matmul(p_[:], wob[:, k, dt * 128:(dt + 1) * 128], yb_[:, k, :], start=(k == 0), stop=(k == KT - 1))
                        return ins
                    cx.op("pe", mm, reads=[wobv, yv_] + ys_, writes=[p_])
                    cx.op("dve", lambda e: e.tensor_tensor(x2_[:, dt, :], p_[:], xb_[:, dt, :], ALU.add), reads=[p_, xb_], writes=[x2_] if dt == 0 else [], pwrites=[x2_] if dt else [])
                    it += 1
                cx.dma("pool", xout[:, ts_].rearrange("(j p) t -> p j t", p=128), x2_[:], x2_, reads=[x2_])
                src_ = x2_
            else:
                src_ = xb_
            cx.op("act", lambda e: e.activation(out=sq_[:], in_=src_[:], func=AF.Square), reads=[src_], writes=[sq_])

            def mmss(e):
                for j in range(4):
                    ins = e.matmul(ssps[:], ones[:], sq_[:, j, :], start=(j == 0), stop=(j == 3))
                return ins
            cx.op("pe", mmss, reads=[ones, sq_], writes=[ssps])
            cx.op("dve", lambda e: e.tensor_copy(ssr_[:], ssps[0:1, :]), reads=[ssps], writes=[ssr_])
            cx.dma("pool", sspD[0:1, ts_], ssr_[:], ssr_, reads=[ssr_])


def emit_pass2(cx, xsrc, ssfD, gnD, lidx, dst, final):
    NB = S // 512
    with cx.scope():
        onesf = cx.sbuf("q_1f", [8, 128], F32)
        cx.op("dve", lambda e: e.memset(onesf[:], 1.0), writes=[onesf])
        g = cx.sbuf("q_g", [128, 12], F32, dma=True)
        cx.dma("sp", g[:], gnD[:, :], g, writes=[g])
        xb = [cx.sbuf("q_xb%d" % i, [128, 4, 512], F32, dma=True) for i in range(2)]
        ssb = [cx.sbuf("q_ssb%d" % i, [8, 512], F32, dma=True) for i in range(2)]
        tot = [cx.psum("q_tot%d" % i, [128, 512], F32) for i in range(2)]
        rstd = [cx.sbuf("q_rstd%d" % i, [128, 512], F32) for i in range(2)]
        ho = [cx.sbuf("q_ho%d" % i, [128, 4, 512], F32 if final else BF16, dma=True) for i in range(2)]
        for tb in range(NB):
            ts_ = slice(tb * 512, (tb + 1) * 512)
            xb_, sb_, t_, r_, h_ = xb[tb % 2], ssb[tb % 2], tot[tb % 2], rstd[tb % 2], ho[tb % 2]
            cx.dma("sp", xb_[:], xsrc[:, ts_].rearrange("(j p) t -> p j t", p=128), xb_, writes=[xb_])
            cx.dma("sp", sb_[:], ssfD[:, ts_], sb_, writes=[sb_])
            cx.op("pe", lambda e: e.matmul(t_[:], onesf[:], sb_[:], start=True, stop=True), reads=[onesf, sb_], writes=[t_])
            cx.op("dve", lambda e: e.tensor_scalar(r_[:], t_[:], 1.0 / D, EPS, ALU.mult, ALU.add), reads=[t_], writes=[r_])
            cx.op("act", lambda e: e.activation(out=r_[:], in_=r_[:], func=AF.Sqrt), reads=[r_], writes=[r_])
            cx.op("dve", lambda e: e.reciprocal(r_[:], r_[:]), reads=[r_], writes=[r_])
            for j in range(4):
                cx.op("dve", lambda e: e.scalar_tensor_tensor(h_[:, j, :], xb_[:, j, :], g[:, lidx * 4 + j:lidx * 4 + j + 1], r_[:], ALU.mult, ALU.mult),
                      reads=[xb_, g, r_], writes=[h_] if j == 0 else [], pwrites=[h_] if j else [])
            cx.dma("pool", dst[:, ts_].rearrange("(j p) t -> p j t", p=128), h_[:], h_, reads=[h_])


def allgather(cx, src, dst):
    cx.barrier()
    if "cc" not in cx.sems:
        cx.sems["cc"] = cx.ges.enter_context(cx.nc.semaphore("cc"))
        cx.cnt["cc"] = 0
    cx.nc.gpsimd.collective_compute("AllGather", ALU.bypass, replica_groups=[list(range(NCORES))],
                                    ins=[src.opt()], outs=[dst.opt()]).then_inc(cx.sems["cc"])
    cx.cnt["cc"] += 1
    cx.barrier()


def build_fused():
    nc = bass.Bass("TRN2", target_bir_lowering=False)
    dt_ = nc.dram_tensor
    xT = dt_("xT", [512, S], F32, kind="ExternalInput").ap()
    gn = dt_("gn", [128, 12], F32, kind="ExternalInput").ap()
    w_in = dt_("w_in", [DEPTH, D, NWC], F32, kind="ExternalInput").ap()
    w_out = dt_("w_out", [DEPTH, D, 512], F32, kind="ExternalInput").ap()
    cwD = dt_("cw", [128, 32], F32, kind="ExternalInput").ap()
    bgD = dt_("bg", [128, 4], F32, kind="ExternalInput").ap()
    mgD = dt_("mg", [128, 4], F32, kind="ExternalInput").ap()
    biasD = dt_("bias", [128, 1536], F32, kind="ExternalInput").ap()
    identD = dt_("ident", [128, 128], F32, kind="ExternalInput").ap()
    triD = dt_("tri", [64, 64], F32, kind="ExternalInput").ap()
    selD = dt_("sel", [8, 512], F32, kind="ExternalInput").ap()
    outT = dt_("outT", [512, S], F32, kind="ExternalOutput").ap()
    x2a = dt_("x2a", [512, S], F32, kind="Internal").ap()
    x2b = dt_("x2b", [512, S], F32, kind="Internal").ap()
    ssp = dt_("ssp", [1, S], F32, kind="Internal").ap()
    ssf = dt_("ssf", [8, S], F32, kind="Internal").ap()
    hTc = dt_("hTc", [512, S], BF16, kind="Internal").ap()
    hTf = dt_("hTf", [D, S], BF16, kind="Internal").ap()
    uT = dt_("uT", [2304, S], BF16, kind="Internal").ap()
    uG = dt_("uG", [2, S], F32, kind="Internal").ap()
    gsc = dt_("gsc", [1, S], F32, kind="Internal").ap()
    yTc = dt_("yTc", [512, S], BF16, kind="Internal").ap()
    yTf = dt_("yTf", [D, S], BF16, kind="Internal").ap()
    ssm = dt_("ssm", [1, S], F32, kind="Internal").ap()
    ssmf = dt_("ssmf", [8, S], F32, kind="Internal").ap()
    with ExitStack() as es:
        cx = Ctx(nc, es)
        emit_pass1(cx, xT, None, ssp, None)
        allgather(cx, ssp, ssf)
        emit_pass2(cx, xT, ssf, gn, 0, hTc, False)
        allgather(cx, hTc, hTf)
        xcur = xT
        for l in range(DEPTH):
            emit_inproj(cx, hTf, w_in[l], uT, uG)
            emit_attention(cx, uT, biasD, identD, yTc)
            emit_mlstm(cx, uT, uG, cwD, bgD, mgD, identD, triD, l, yTc, ssm, gsc)
            allgather(cx, yTc, yTf)
            allgather(cx, ssm, ssmf)
            xnew = x2a if l == 0 else x2b
            emit_pass1(cx, xcur, xnew, ssp, dict(yTf=yTf, ssmf=ssmf, wo=w_out[l], selD=selD))
            allgather(cx, ssp, ssf)
            last = (l == DEPTH - 1)
            emit_pass2(cx, xnew, ssf, gn, l + 1, outT if last else hTc, last)
            if not last:
                allgather(cx, hTc, hTf)
            xcur = xnew
        cx.barrier()
    return nc


_NC_CACHE = {}


def kernel(x, norm_g, w_in, b_gate, conv_w, mlstm_norm_g, w_out, rel_bias, final_g):
    x = np.asarray(x, np.float32)
    norm_g, w_in, b_gate, conv_w = (np.asarray(a, np.float32) for a in (norm_g, w_in, b_gate, conv_w))
    mlstm_norm_g, w_out, rel_bias, final_g = (np.asarray(a, np.float32) for a in (mlstm_norm_g, w_out, rel_bias, final_g))
    consts, pc = prep_common(norm_g, final_g, b_gate, conv_w, mlstm_norm_g, rel_bias)
    sel = np.zeros((8, 4, 128), np.float32)
    for hd in range(4):
        sel[2 * hd:2 * hd + 2, hd, :] = 1.0
    yo = ycol_order()
    xTfull = x[0].T
    in_maps = []
    for c in range(NCORES):
        cols = core_cols(c)
        rows = slice(512 * c, 512 * (c + 1))
        gn_c = np.ascontiguousarray(consts["gn"].reshape(128, 3, 32)[:, :, 4 * c:4 * c + 4]).reshape(128, 12)
        in_maps.append({
            "xT": np.ascontiguousarray(xTfull[rows]),
            "gn": gn_c,
            "w_in": np.ascontiguousarray(w_in[:, :, cols]),
            "w_out": np.ascontiguousarray(w_out[:, yo, :][:, :, rows]),
            "cw": pc[c]["cw"], "bg": pc[c]["bg"], "mg": pc[c]["mg"], "bias": pc[c]["bias"],
            "ident": consts["ident"], "tri": consts["tri"], "sel": sel.reshape(8, 512),
        })
    if "nc" not in _NC_CACHE:
        _NC_CACHE["nc"] = build_fused()
    res = run_bass_kernel_spmd(_NC_CACHE["nc"], in_maps, core_ids=list(range(NCORES)))
    outT = np.concatenate([res.results[c]["outT"] for c in range(NCORES)], axis=0)
    return np.ascontiguousarray(outT.T)[None].astype(np.float32)
```

```python
import numpy as np
from contextlib import ExitStack
import concourse.bass as bass
import concourse.mybir as mybir
from concourse.bass_utils import run_bass_kernel_spmd

F32 = mybir.dt.float32
BF16 = mybir.dt.bfloat16
AF = mybir.ActivationFunctionType
ALU = mybir.AluOpType
AX = mybir.AxisListType

NCORES = 8
D = 4096
S = 8192
TOK = S // NCORES
KT = D // 128
DEPTH = 2
EPS = 1e-6
NWC = 2306
PATTERNS = ((128, 1), (512, 4), (2048, 16))
NEG = -30000.0


class Buf:
    def __init__(self, ctx, t, name):
        self.ctx = ctx
        self.t = t
        self.name = name
        self.w = {}
        self.r = {}
        self.dsem = None

    def __getitem__(self, k):
        return self.t[k]

    def ap(self):
        return self.t[:]


def _merge(dst, src):
    for k, v in src.items():
        if dst.get(k, 0) < v:
            dst[k] = v


class Ctx:
    def __init__(self, nc, es):
        self.nc = nc
        self.es = es
        self.eng = {"pe": nc.tensor, "act": nc.scalar, "dve": nc.vector, "pool": nc.gpsimd, "sp": nc.sync}
        self.sems = {}
        self.cnt = {}
        self.waited = {k: {} for k in self.eng}
        for k in self.eng:
            self.sems[k] = es.enter_context(nc.semaphore("e_" + k))
            self.cnt[k] = 0
        self.ges = es
        self.dn = {"s": 0, "h": 0}
        self.uid = 0

    def scope(self):
        return _Scope(self)

    def barrier(self):
        for e in self.eng:
            need = {k: v for k, v in self.cnt.items() if v > 0 and k != e}
            self._do_waits(e, need)

    def sbuf(self, name, shape, dt, dma=False):
        self.uid += 1
        name = "%s_%d" % (name, self.uid)
        t = self.es.enter_context(self.nc.sbuf_tensor(name, list(shape), dt))
        b = Buf(self, t, name)
        if dma:
            self.add_dsem(b)
        return b

    def psum(self, name, shape, dt=F32):
        self.uid += 1
        name = "%s_%d" % (name, self.uid)
        t = self.es.enter_context(self.nc.psum_tensor(name, list(shape), dt))
        return Buf(self, t, name)

    def add_dsem(self, b):
        b.dsem = {}

    def _dkey(self, b, q):
        kind = "s" if q == "pool" else "h"
        if kind not in b.dsem:
            key = "%s%d" % (kind, self.dn[kind])
            self.dn[kind] += 1
            if key not in self.sems:
                self.sems[key] = self.ges.enter_context(self.nc.semaphore(key))
                self.cnt[key] = 0
            b.dsem[kind] = key
        return b.dsem[kind]

    def virt(self, name, dma=False):
        b = Buf(self, None, name)
        if dma:
            self.add_dsem(b)
        return b

    def _do_waits(self, e, need):
        eng = self.eng[e]
        wd = self.waited[e]
        for k, v in need.items():
            if wd.get(k, 0) < v:
                eng.wait_ge(self.sems[k], v)
                wd[k] = v

    def _hazards(self, reads, writes, pwrites):
        need = {}
        for b in reads:
            _merge(need, b.w)
        for b in writes:
            _merge(need, b.w)
            _merge(need, b.r)
        for b in pwrites:
            _merge(need, b.r)
        return need

    def _record(self, ev, reads, writes, pwrites):
        for b in reads:
            _merge(b.r, ev)
        for b in writes:
            b.w = dict(ev)
            b.r = {}
        for b in pwrites:
            _merge(b.w, ev)

    def op(self, e, fn, reads=(), writes=(), pwrites=(), same_eng_ok=False):
        need = self._hazards(reads, writes, pwrites)
        if same_eng_ok:
            need.pop(e, None)
        self._do_waits(e, need)
        ins = fn(self.eng[e])
        self.cnt[e] += 1
        ins.then_inc(self.sems[e], 1)
        self._record({e: self.cnt[e]}, reads, writes, pwrites)

    def dma(self, q, out_ap, in_ap, sem_buf, reads=(), writes=(), pwrites=(), n=1, **kw):
        need = self._hazards(reads, writes, pwrites)
        self._do_waits(q, need)
        key = self._dkey(sem_buf, q)
        outs = out_ap if isinstance(out_ap, list) else [out_ap]
        ins_ = in_ap if isinstance(in_ap, list) else [in_ap]
        for o, i in zip(outs, ins_):
            self.eng[q].dma_start(out=o, in_=i, **kw).then_inc(self.sems[key], 16)
            self.cnt[key] += 16
        self._record({key: self.cnt[key]}, reads, writes, pwrites)

    def wait_all(self, e, bufs):
        need = {}
        for b in bufs:
            _merge(need, b.w)
            _merge(need, b.r)
        self._do_waits(e, need)


class _Scope:
    def __init__(self, cx):
        self.cx = cx

    def __enter__(self):
        self.old = self.cx.es
        self.cx.es = ExitStack()
        self.dn_saved = dict(self.cx.dn)
        return self.cx

    def __exit__(self, *a):
        self.cx.barrier()
        self.cx.es.close()
        self.cx.es = self.old
        self.cx.dn = self.dn_saved
        return False


def rr(cx, name, n):
    return [cx.virt("%s%d" % (name, i), dma=True) for i in range(n)]


def emit_norm(cx, xT, gn, lidx, hT, final=False):
    with cx.scope():
        ones = cx.sbuf("n_ones", [128, 128], BF16)
        g = cx.sbuf("n_g", [128, 96], F32, dma=True)
        xs = cx.sbuf("n_xs", [128, KT, TOK], F32)
        sq = [cx.sbuf("n_sq%d" % i, [128, TOK], BF16) for i in range(2)]
        ss = cx.psum("n_ss", [128, TOK], F32)
        rstd = cx.sbuf("n_rstd", [128, TOK], F32)
        ho = [cx.sbuf("n_ho%d" % i, [128, TOK], F32 if final else BF16, dma=True) for i in range(3)]
        cx.op("dve", lambda e: e.memset(ones[:], 1.0), writes=[ones])
        cx.dma("sp", g[:], gn[:, :], g, writes=[g])
        xsk = [cx.virt("n_xs%d" % k) for k in range(KT)]
        slots = rr(cx, "n_ld", 4)
        for k in range(KT):
            cx.dma("sp", xs[:, k, :], xT[k * 128:(k + 1) * 128, :], slots[k % 4], writes=[slots[k % 4], xsk[k]])
        for k in range(KT):
            s_ = sq[k % 2]
            cx.op("act", lambda e: e.activation(out=s_[:], in_=xs[:, k, :], func=AF.Square), reads=[xsk[k]], writes=[s_])

            def mm(e):
                e.matmul(ss[:, 0:512], ones[:], s_[:, 0:512], start=(k == 0), stop=(k == KT - 1))
                return e.matmul(ss[:, 512:1024], ones[:], s_[:, 512:1024], start=(k == 0), stop=(k == KT - 1))
            cx.op("pe", mm, reads=[ones, s_], writes=[ss] if k == 0 else [], pwrites=[ss] if k else [])
        cx.op("dve", lambda e: e.tensor_scalar(rstd[:], ss[:], 1.0 / D, EPS, ALU.mult, ALU.add), reads=[ss], writes=[rstd])
        cx.op("act", lambda e: e.activation(out=rstd[:], in_=rstd[:], func=AF.Sqrt), reads=[rstd], writes=[rstd])
        cx.op("dve", lambda e: e.reciprocal(rstd[:], rstd[:]), reads=[rstd], writes=[rstd])
        for k in range(KT):
            h_ = ho[k % 3]
            cx.op("dve", lambda e: e.scalar_tensor_tensor(h_[:], xs[:, k, :], g[:, lidx * 32 + k:lidx * 32 + k + 1], rstd[:], ALU.mult, ALU.mult),
                  reads=[xsk[k], g, rstd], writes=[h_])
            cx.dma("pool", hT[k * 128:(k + 1) * 128, :], h_[:], h_, reads=[h_])


def emit_inproj(cx, hTf, w, uT, uG):
    QS = 128.0 ** -0.5
    for grp, (c0, c1) in enumerate(((0, 1024), (1024, NWC))):
        ncol = c1 - c0
        nct = ncol // 128
        with cx.scope():
            wb = cx.sbuf("a_wb", [128, KT, ncol], BF16)
            st = [cx.sbuf("a_st%d" % i, [128, ncol], F32, dma=True) for i in range(4)]
            hb = [cx.sbuf("a_hb%d" % i, [128, KT, 512], BF16) for i in range(2)]
            hbs = [rr(cx, "a_hbs%d_" % i, 4) for i in range(2)]
            ob = [cx.sbuf("a_ob%d" % i, [128, 512], BF16, dma=True) for i in range(4)]
            og = [cx.sbuf("a_og%d" % i, [2, 512], F32, dma=True) for i in range(2)]
            ps = [cx.psum("a_ps%d" % i, [128, 512], F32) for i in range(6)]
            wbv = cx.virt("a_wbv")
            for k in range(KT):
                s_ = st[k % 4]
                cx.dma("sp" if k % 2 == 0 else "act", s_[:], w[k * 128:(k + 1) * 128, c0:c1], s_, writes=[s_])
                en = ("act", "dve", "pool")[k % 3]
                if en == "act":
                    cx.op(en, lambda e: e.copy(wb[:, k, :], s_[:]), reads=[s_], pwrites=[wbv])
                else:
                    cx.op(en, lambda e: e.tensor_copy(wb[:, k, :], s_[:]), reads=[s_], pwrites=[wbv])
            it = 0
            for tb in range(S // 512):
                off = tb * 512
                h_ = hb[tb % 2]
                hv = hbs[tb % 2]
                src = hTf.rearrange("(k p) t -> p k t", p=128)
                for q in range(4):
                    cx.dma("sp", h_[:, q * 8:(q + 1) * 8, :], src[:, q * 8:(q + 1) * 8, off:off + 512], hv[q], writes=[hv[q]])
                for ct in range(nct):
                    p_ = ps[it % 6]

                    def mm(e):
                        for k in range(KT):
                            ins = e.matmul(p_[:], wb[:, k, ct * 128:(ct + 1) * 128], h_[:, k, :], start=(k == 0), stop=(k == KT - 1))
                        return ins
                    cx.op("pe", mm, reads=[wbv] + hv, writes=[p_])
                    o_ = ob[it % 4]
                    scale = QS if (grp == 0 and ct < 2) else 1.0
                    if it % 2 == 0:
                        cx.op("act", lambda e: e.activation(out=o_[:], in_=p_[:], func=AF.Copy, scale=scale), reads=[p_], writes=[o_])
                    else:
                        cx.op("dve", lambda e: e.tensor_scalar(o_[:], p_[:], scale, None, ALU.mult), reads=[p_], writes=[o_])
                    gct = c0 // 128 + ct
                    cx.dma("pool", uT[gct * 128:(gct + 1) * 128, tb * 512:(tb + 1) * 512], o_[:], o_, reads=[o_])
                    it += 1
                if grp == 1:
                    p_ = ps[it % 6]

                    def mmg(e):
                        for k in range(KT):
                            ins = e.matmul(p_[0:2, :], wb[:, k, ncol - 2:ncol], h_[:, k, :], start=(k == 0), stop=(k == KT - 1))
                        return ins
                    cx.op("pe", mmg, reads=[wbv] + hv, writes=[p_])
                    o_ = og[tb % 2]
                    cx.op("dve", lambda e: e.tensor_copy(o_[:], p_[0:2, :]), reads=[p_], writes=[o_])
                    cx.dma("pool", uG[:, tb * 512:(tb + 1) * 512], o_[:], o_, reads=[o_])
                    it += 1


def core_cols(c):
    hd, half = c // 2, c % 2
    a = np.arange(256)
    cols = []
    for g in range(4):
        cols.append(g * 2048 + 256 * c + a)
    base = 8192
    cols.append(base + 256 * hd + a)
    cols.append(base + 1024 + 256 * hd + a)
    base2 = 8192 + 2048
    for g in range(3):
        cols.append(base2 + g * 2048 + 512 * hd + 256 * half + a)
    cols.append(np.array([16384 + hd]))
    cols.append(np.array([16388 + hd]))
    return np.concatenate(cols)


def ycol_order():
    idx = []
    for c in range(NCORES):
        idx.append(256 * c + np.arange(256))
        idx.append(2048 + 256 * c + np.arange(256))
    return np.concatenate(idx)


def rel_buckets_np(dist):
    n = dist.astype(np.float32)
    large = 16 + np.floor(np.log(np.maximum(n, 1.0) / 16) / np.log(2048 / 16) * 16)
    large = np.minimum(large, 31)
    return np.where(n < 16, n, large).astype(np.int32)


def bias_tables(rel_bias, c):
    rb = np.concatenate([rel_bias, np.full((1, rel_bias.shape[1]), NEG, np.float32)], axis=0)
    i = np.arange(128)[:, None]
    j = np.arange(128)[None, :]
    out = np.zeros((128, 2, 3, 256), np.float32)
    for pi, (wd, d) in enumerate(PATTERNS):
        o_prev = j - i + 128
        o_diag = j - i
        b_prev = np.where(o_prev <= 128, rel_buckets_np(np.maximum(o_prev, 0) * d), 32)
        b_diag = np.where(o_diag >= 0, rel_buckets_np(np.maximum(o_diag, 0) * d), 32)
        for a in range(2):
            out[:, a, pi, 0:128] = rb[b_prev, 2 * c + a]
            out[:, a, pi, 128:256] = rb[b_diag, 2 * c + a]
    return out


def prep_common(norm_g, final_g, b_gate, conv_w, mlstm_norm_g, rel_bias):
    gn = np.zeros((128, 96), np.float32)
    for l in range(DEPTH):
        gn[:, l * 32:(l + 1) * 32] = norm_g[l].reshape(32, 128).T
    gn[:, 64:96] = final_g.reshape(32, 128).T
    per_core = []
    tri = np.triu(np.ones((64, 64), np.float32))
    for c in range(NCORES):
        hd, half = c // 2, c % 2
        cw = np.zeros((128, DEPTH, 4, 4), np.float32)
        bg = np.zeros((128, DEPTH, 2), np.float32)
        mg = np.zeros((128, DEPTH, 2), np.float32)
        for l in range(DEPTH):
            for ti in range(4):
                base = (hd * 256 + ti * 128) if ti < 2 else (1024 + hd * 256 + (ti - 2) * 128)
                cw[:, l, ti, :] = conv_w[l][:, base:base + 128].T
            bg[:, l, 0] = b_gate[l, hd]
            bg[:, l, 1] = b_gate[l, 4 + hd]
            for j in range(2):
                b0 = hd * 512 + half * 256 + j * 128
                mg[:, l, j] = mlstm_norm_g[l, b0:b0 + 128]
        per_core.append(dict(cw=cw.reshape(128, -1), bg=bg.reshape(128, -1), mg=mg.reshape(128, -1),
                             bias=bias_tables(rel_bias, c).reshape(128, -1)))
    consts = dict(gn=gn, ident=np.eye(128, dtype=np.float32), tri=tri)
    return consts, per_core


def emit_attention(cx, uT, biasD, identD, yT, pats=(0, 1, 2), heads=(0, 1)):
    with cx.scope():
        identf = cx.sbuf("b_idf", [128, 128], F32, dma=True)
        ident = cx.sbuf("b_id", [128, 128], BF16)
        ones = cx.sbuf("b_ones", [128, 128], BF16)
        biasf = cx.sbuf("b_bf", [128, 2 * 3 * 256], F32, dma=True)
        biasb = cx.sbuf("b_bb", [128, 2 * 3 * 256], BF16)
        cx.dma("sp", identf[:], identD[:, :], identf, writes=[identf])
        cx.dma("sp", biasf[:], biasD[:, :], biasf, writes=[biasf])
        cx.op("dve", lambda e: e.tensor_copy(ident[:], identf[:]), reads=[identf], writes=[ident])
        cx.op("dve", lambda e: e.tensor_copy(biasb[:], biasf[:]), reads=[biasf], writes=[biasb])
        cx.op("dve", lambda e: e.memset(ones[:], 1.0), writes=[ones])
        qT, kT, vT, zT = [cx.sbuf("b_" + n, [128, S], BF16, dma=True) for n in "qkvz"]
        acc = cx.sbuf("b_acc", [128, 2, S], F32)
        accb = cx.virt("b_accv")
        vtok = [cx.sbuf("b_vt%d" % i, [128, 128], BF16) for i in range(3)]
        pT = [cx.sbuf("b_pT%d" % i, [128, 256], BF16) for i in range(2)]
        sps = [cx.psum("b_sp%d" % i, [128, 256], F32) for i in range(2)]
        nds = [cx.psum("b_nd%d" % i, [128, 2, 128], F32) for i in range(2)]
        tps = [cx.psum("b_tp%d" % i, [128, 128], BF16) for i in range(2)]
        rec = cx.sbuf("b_rec", [128, 2048], F32)
        sz = cx.sbuf("b_sz", [128, 2048], F32)
        yo = [cx.sbuf("b_yo%d" % i, [128, 2048], BF16, dma=True) for i in range(2)]
        it = 0
        pending = None
        for a in heads:
            for n_, (t_, r0) in enumerate(((qT, 0), (kT, 256), (vT, 512), (zT, 768))):
                cx.dma("sp", t_[:], uT[r0 + a * 128:r0 + (a + 1) * 128, :], t_, writes=[t_])
            for pi, (wd, d) in enumerate(PATTERNS):
                if pi not in pats:
                    continue
                nqb = S // (128 * d)
                bofs = (a * 3 + pi) * 256
                for r in range(d):
                    def sl(b):
                        st_ = r + d * 128 * b
                        return slice(st_, st_ + d * 127 + 1, d)
                    for qb in range(nqb):
                        tp_, vt_ = tps[it % 2], vtok[it % 3]
                        cx.op("pe", lambda e: e.transpose(tp_[:], vT[:, sl(qb)], ident[:]), reads=[vT, ident], writes=[tp_])
                        cx.op("act", lambda e: e.copy(vt_[:], tp_[:]), reads=[tp_], writes=[vt_])
                        sp_, p_, nd_ = sps[it % 2], pT[it % 2], nds[it % 2]
                        c0 = 0 if qb > 0 else 128

                        def mm1(e, sp_=sp_, qb=qb, c0=c0, sl=sl):
                            e.matmul(sp_[:, 128:256], kT[:, sl(qb)], qT[:, sl(qb)], start=True, stop=False)
                            if qb > 0:
                                e.matmul(sp_[:, 0:128], kT[:, sl(qb - 1)], qT[:, sl(qb)], start=False, stop=False)
                            return e.matmul(sp_[:, c0:256], ident[:], biasb[:, bofs + c0:bofs + 256], start=False, stop=True)
                        cx.op("pe", mm1, reads=[kT, qT, ident, biasb], writes=[sp_])
                        cx.op("act", lambda e: e.activation(out=p_[:, c0:256], in_=sp_[:, c0:256], func=AF.Exp), reads=[sp_], writes=[p_])
                        if pending is not None:
                            pending()
                        vprev = vtok[(it - 1) % 3]
                        asl = acc[:, :, sl(qb)]

                        def stage2(qb=qb, p_=p_, nd_=nd_, vt_=vt_, vprev=vprev, asl=asl, pi=pi):
                            def mm2(e):
                                if qb > 0:
                                    e.matmul(nd_[:, 0, :], vprev[:], p_[:, 0:128], start=True, stop=False)
                                e.matmul(nd_[:, 0, :], vt_[:], p_[:, 128:256], start=(qb == 0), stop=True)
                                if qb > 0:
                                    e.matmul(nd_[:, 1, :], ones[:], p_[:, 0:128], start=True, stop=False)
                                return e.matmul(nd_[:, 1, :], ones[:], p_[:, 128:256], start=(qb == 0), stop=True)
                            cx.op("pe", mm2, reads=[p_, vt_, vprev, ones], writes=[nd_])
                            if pi == pats[0]:
                                cx.op("dve", lambda e: e.tensor_copy(asl, nd_[:]), reads=[nd_], pwrites=[accb])
                            else:
                                cx.op("dve", lambda e: e.tensor_tensor(asl, nd_[:], asl, ALU.add), reads=[nd_], pwrites=[accb])
                        pending = stage2
                        it += 1
            if pending is not None:
                pending()
                pending = None
            for ch in range(S // 2048):
                cs_ = slice(ch * 2048, (ch + 1) * 2048)
                y_ = yo[ch % 2]
                cx.op("dve", lambda e: e.reciprocal(rec[:], acc[:, 1, cs_]), reads=[accb], writes=[rec])
                cx.op("dve", lambda e: e.tensor_tensor(rec[:], acc[:, 0, cs_], rec[:], ALU.mult), reads=[accb, rec], writes=[rec])
                cx.op("act", lambda e: e.activation(out=sz[:], in_=zT[:, cs_], func=AF.Silu), reads=[zT], writes=[sz])
                cx.op("dve", lambda e: e.tensor_tensor(y_[:], rec[:], sz[:], ALU.mult), reads=[rec, sz], writes=[y_])
                cx.dma("pool", yT[ch * 4:(ch + 1) * 4, :, a, :].rearrange("b p t -> p b t"), y_[:].rearrange("p (b t) -> p b t", b=4), y_, reads=[y_])


def emit_mlstm(cx, uT, uG, cwD, bgD, mgD, identD, triD, l, yT, ssm, gsc):
    L = 64
    NCH = S // L
    LN16 = float(np.log(16.0))
    with cx.scope():
        identf = cx.sbuf("m_idf", [128, 128], F32, dma=True)
        ident = cx.sbuf("m_id", [128, 128], BF16)
        onesf = cx.sbuf("m_1f", [128, 128], F32)
        ones = cx.sbuf("m_1b", [128, 128], BF16)
        tri = cx.sbuf("m_tri", [64, 64], F32, dma=True)
        cw = cx.sbuf("m_cw", [128, 32], F32, dma=True)
        bg = cx.sbuf("m_bg", [128, 4], F32, dma=True)
        mg = cx.sbuf("m_mg", [128, 4], F32, dma=True)
        for t_, d_ in ((identf, identD), (tri, triD), (cw, cwD), (bg, bgD), (mg, mgD)):
            cx.dma("sp", t_[:], d_[:, :], t_, writes=[t_])
        cx.op("dve", lambda e: e.tensor_copy(ident[:], identf[:]), reads=[identf], writes=[ident])
        cx.op("dve", lambda e: e.memset(onesf[:], 1.0), writes=[onesf])
        cx.op("dve", lambda e: e.memset(ones[:], 1.0), writes=[ones])
        a_bc = cx.sbuf("m_abc", [128, 128], F32)
        wT = cx.sbuf("m_wT", [64, 128], F32)
        gscv = cx.virt("m_gscv")
        with cx.scope():
            ig = cx.sbuf("g_ig", [128, L], F32, dma=True)
            fg = cx.sbuf("g_fg", [128, L], F32, dma=True)
            cx.dma("sp", ig[:], uG[0:1, :].rearrange("o (c s) -> (o c) s", s=L), ig, writes=[ig])
            cx.dma("sp", fg[:], uG[1:2, :].rearrange("o (c s) -> (o c) s", s=L), fg, writes=[fg])
            li = cx.sbuf("g_li", [128, L], F32)
            nbf = cx.sbuf("g_nbf", [128, 1], F32)
            e1 = cx.sbuf("g_e1", [128, L], F32)
            cs = cx.sbuf("g_cs", [128, L], F32)
            dd = cx.sbuf("g_dd", [128, L], F32)
            cols2 = cx.sbuf("g_c2", [128, 2], F32)
            rows = cx.sbuf("g_rows", [1, 640], F32)
            psr = cx.psum("g_psr", [128, 256], F32)
            Mcol = cx.sbuf("g_M", [128, 2], F32)
            Wcs = cx.sbuf("g_W", [128, L], F32)
            Rcl = cx.sbuf("g_R", [128, L], F32, dma=True)
            cx.op("dve", lambda e: e.tensor_scalar(li[:], ig[:], bg[:, 2 * l:2 * l + 1], None, ALU.add), reads=[ig, bg], writes=[li])
            cx.op("dve", lambda e: e.tensor_scalar(nbf[:], bg[:, 2 * l + 1:2 * l + 2], -1.0, None, ALU.mult), reads=[bg], writes=[nbf])
            cx.op("act", lambda e: e.activation(out=e1[:], in_=fg[:], func=AF.Exp, bias=nbf[:, 0:1], scale=-1.0), reads=[fg, nbf], writes=[e1])
            cx.op("dve", lambda e: e.tensor_scalar(e1[:], e1[:], 1.0, None, ALU.add), reads=[e1], writes=[e1])
            cx.op("act", lambda e: e.activation(out=e1[:], in_=e1[:], func=AF.Ln), reads=[e1], writes=[e1])
            cx.op("dve", lambda e: e.tensor_tensor_scan(cs[:], onesf[:, 0:L], e1[:], 0.0, ALU.mult, ALU.add), reads=[e1, onesf], writes=[cs])
            cx.op("dve", lambda e: e.tensor_tensor(dd[:], li[:], cs[:], ALU.add), reads=[li, cs], writes=[dd])
            cx.op("dve", lambda e: e.reduce_max(cols2[:, 0:1], dd[:], AX.X), reads=[dd], writes=[cols2])
            cx.op("dve", lambda e: e.tensor_scalar(cols2[:, 1:2], cs[:, L - 1:L], -1.0, None, ALU.mult), reads=[cs, cols2], writes=[cols2])

            def mmr(e):
                e.matmul(psr[0:1, 0:128], cols2[:, 0:1], identf[:], start=True, stop=True)
                return e.matmul(psr[0:1, 128:256], cols2[:, 1:2], identf[:], start=False, stop=True)
            cx.op("pe", mmr, reads=[cols2, identf], writes=[psr])
            cx.op("dve", lambda e: e.tensor_copy(rows[:, 0:256], psr[0:1, 0:256]), reads=[psr], writes=[rows])
            cx.op("dve", lambda e: e.tensor_tensor(rows[:, 256:384], rows[:, 0:128], rows[:, 128:256], ALU.add), reads=[rows], writes=[rows])
            cx.op("dve", lambda e: e.tensor_tensor_scan(rows[:, 384:512], rows[:, 128:256], rows[:, 256:384], 0.0, ALU.add, ALU.max), reads=[rows], writes=[rows])
            cx.op("dve", lambda e: e.memset(rows[:, 512:513], 0.0), reads=[rows], writes=[rows])
            cx.op("dve", lambda e: e.tensor_copy(rows[:, 513:640], rows[:, 384:511]), reads=[rows], writes=[rows])
            cx.op("dve", lambda e: e.tensor_tensor(rows[:, 0:128], rows[:, 512:640], rows[:, 0:128], ALU.max), reads=[rows], writes=[rows])
            cx.op("dve", lambda e: e.tensor_tensor(rows[:, 128:256], rows[:, 512:640], rows[:, 0:128], ALU.subtract), reads=[rows], writes=[rows])
            cx.op("act", lambda e: e.activation(out=rows[:, 128:256], in_=rows[:, 128:256], func=AF.Exp), reads=[rows], writes=[rows])

            def mmc(e):
                e.matmul(psr[:, 0:1], rows[0:1, 0:128], onesf[0:1, 0:1], start=True, stop=True)
                return e.matmul(psr[:, 128:256], onesf[0:1, :], rows[0:1, 128:256], start=False, stop=True)
            cx.op("pe", mmc, reads=[rows, onesf], writes=[psr])
            cx.op("dve", lambda e: e.tensor_copy(Mcol[:, 0:1], psr[:, 0:1]), reads=[psr], writes=[Mcol])
            cx.op("dve", lambda e: e.tensor_copy(a_bc[:], psr[:, 128:256]), reads=[psr], writes=[a_bc])
            cx.op("dve", lambda e: e.tensor_scalar(Mcol[:, 1:2], Mcol[:, 0:1], -1.0, -LN16, ALU.mult, ALU.add), reads=[Mcol], writes=[Mcol])
            cx.op("act", lambda e: e.activation(out=Wcs[:], in_=dd[:], func=AF.Exp, bias=Mcol[:, 1:2], scale=1.0), reads=[dd, Mcol], writes=[Wcs])
            cx.op("act", lambda e: e.activation(out=Rcl[:], in_=cs[:], func=AF.Exp, bias=Mcol[:, 0:1], scale=-1.0), reads=[cs, Mcol], writes=[Rcl])
            cx.op("pe", lambda e: e.matmul(psr[0:64, 0:128], Wcs[:], identf[:], start=True, stop=True), reads=[Wcs, identf, a_bc, Mcol], writes=[psr])
            cx.op("dve", lambda e: e.tensor_copy(wT[:], psr[0:64, 0:128]), reads=[psr], writes=[wT])
            cx.dma("sp", gsc[0:1, :].rearrange("o (c s) -> (o c) s", s=L), Rcl[:], Rcl, reads=[Rcl], writes=[gscv])
        qk = cx.sbuf("m_qk", [128, 4, S], BF16)
        qkv = cx.virt("m_qkv")
        with cx.scope():
            dg = cx.sbuf("c_dg", [128, 16, 128], BF16)
            for ti in range(4):
                for j in range(4):
                    col = l * 16 + ti * 4 + j
                    cx.op("dve", lambda e: e.tensor_scalar(dg[:, ti * 4 + j, :], identf[:], cw[:, col:col + 1], None, ALU.mult), reads=[identf, cw], pwrites=[dg])
            pre = [cx.sbuf("c_pre%d" % i, [128, S + 3], BF16, dma=True) for i in range(2)]
            cps = [cx.psum("c_ps%d" % i, [128, 512], F32) for i in range(4)]
            it = 0
            for ti in range(4):
                r0 = 1024 + ti * 128
                p_ = pre[ti % 2]
                cx.op("pool", lambda e: e.memset(p_[:, 0:3], 0.0), writes=[p_])
                cx.dma("sp", p_[:, 3:S + 3], uT[r0:r0 + 128, :], p_, reads=[p_], pwrites=[p_])
                for tb in range(S // 512):
                    t0 = tb * 512
                    c_ = cps[it % 4]

                    def mmc(e):
                        for j in range(4):
                            ins = e.matmul(c_[:], dg[:, ti * 4 + j, :], p_[:, t0 + j:t0 + j + 512], start=(j == 0), stop=(j == 3))
                        return ins
                    cx.op("pe", mmc, reads=[dg, p_], writes=[c_])
                    cx.op("act", lambda e: e.activation(out=qk[:, ti, t0:t0 + 512], in_=c_[:], func=AF.Silu), reads=[c_], pwrites=[qkv])
                    it += 1
        vT = cx.sbuf("m_vT", [128, 2, S], BF16, dma=True)
        cx.dma("sp", vT[:], uT[1536:1792, :].rearrange("(j p) t -> p j t", p=128), vT, writes=[vT])
        Sf = cx.sbuf("m_Sf", [128, 2, 384], F32)
        Sb = [cx.sbuf("m_Sb%d" % i, [128, 2, 384], BF16) for i in range(2)]
        kw = [cx.sbuf("m_kw%d" % i, [64, 256], BF16) for i in range(2)]
        vext = [cx.sbuf("m_vx%d" % i, [64, 384], BF16) for i in range(2)]
        sq = [cx.sbuf("m_sq%d" % i, [64, 64], BF16) for i in range(2)]
        qa = [cx.sbuf("m_qa%d" % i, [128, 2, 64], BF16) for i in range(2)]
        xyg = [cx.sbuf("m_xy%d" % i, [128, 3, 512], F32) for i in range(2)]
        tpp = [cx.psum("m_tp%d" % i, [64, 4, 128], BF16) for i in range(2)]
        spp1 = cx.psum("m_sp", [64, 64], F32)
        spp = [spp1, spp1]
        o3 = [cx.psum("m_o3%d" % i, [128, 3, 64], F32) for i in range(2)]
        U1 = cx.psum("m_U", [128, 2, 512], F32)
        U = [U1, U1]
        ssp = cx.psum("m_ssp", [128, 512], F32)
        rb = [cx.sbuf("f_rb%d" % i, [128, 512], F32, dma=True) for i in range(2)]
        oz = [cx.sbuf("f_oz%d" % i, [128, 4, 512], BF16, dma=True) for i in range(2)]
        t1 = cx.sbuf("f_t1", [128, 512], F32)
        hT = cx.sbuf("f_hT", [128, 2, 512], F32)
        sqh = cx.sbuf("f_sqh", [128, 2, 512], BF16)
        ssr = [cx.sbuf("f_ssr%d" % i, [1, 512], F32, dma=True) for i in range(2)]
        sg = cx.sbuf("f_sg", [128, 4, 512], F32)
        tz = cx.sbuf("f_tz", [128, 2, 512], F32)
        yo = [cx.sbuf("f_yo%d" % i, [128, 2, 512], BF16, dma=True) for i in range(2)]
        cx.op("dve", lambda e: e.memset(Sf[:], 0.0), writes=[Sf])
        cx.op("dve", lambda e: e.memset(Sb[0][:], 0.0), writes=[Sb[0]])
        for i in range(2):
            cx.op("pool", lambda e: e.memset(vext[i][:], 1.0), writes=[vext[i]])
        def stage_a(c):
            tok = slice(c * L, (c + 1) * L)
            i2 = c % 2
            tp_, kw_, vx_, sp_, sq_, qa_ = tpp[i2], kw[i2], vext[i2], spp[i2], sq[i2], qa[i2]

            def tr(e):
                for i in range(4):
                    src = qk[:, 2 + i, tok] if i < 2 else vT[:, i - 2, tok]
                    ins = e.transpose(tp_[:, i, :], src, ident[:])
                return ins
            cx.op("pe", tr, reads=[qkv, vT, ident], writes=[tp_])
            cx.op("act", lambda e: e.activation(out=kw_[:].rearrange("p (a b) -> p a b", a=2), in_=tp_[:, 0:2, :], func=AF.Copy, scale=wT[:, c:c + 1]),
                  reads=[tp_, wT], writes=[kw_])
            cx.op("dve", lambda e: e.tensor_copy(vx_[:, 0:256].rearrange("p (a b) -> p a b", a=2), tp_[:, 2:4, :]), reads=[tp_], pwrites=[vx_])

            def mms(e):
                e.matmul(sp_[:], qk[:, 2, tok], qk[:, 0, tok], start=True, stop=False)
                return e.matmul(sp_[:], qk[:, 3, tok], qk[:, 1, tok], start=False, stop=True)
            cx.op("pe", mms, reads=[qkv], writes=[sp_])
            cx.op("dve", lambda e: e.scalar_tensor_tensor(sq_[:], sp_[:], wT[:, c:c + 1], tri[:], ALU.mult, ALU.mult), reads=[sp_, wT, tri], writes=[sq_])
            cx.op("pool", lambda e: e.tensor_scalar(qa_[:], qk[:, 0:2, tok], a_bc[:, c:c + 1], None, ALU.mult), reads=[qkv, a_bc], writes=[qa_])

        stage_a(0)
        for c in range(NCH):
            if c + 1 < NCH:
                stage_a(c + 1)
            i2 = c % 2
            kw_, vx_, sq_, qa_, o3_ = kw[i2], vext[i2], sq[i2], qa[i2], o3[i2]
            Sc, Sn = Sb[i2], Sb[1 - i2]
            g = c // 8
            xy_ = xyg[g % 2]
            U_ = U[i2]

            def mmu(e):
                e.matmul(U_[:, 0, 0:384], kw_[:, 0:128], vx_[:], start=True, stop=True)
                return e.matmul(U_[:, 1, 0:384], kw_[:, 128:256], vx_[:], start=True, stop=True)
            cx.op("pe", mmu, reads=[kw_, vx_], writes=[U_])

            def mmo(e):
                for j3 in range(3):
                    cs_ = slice(j3 * 128, (j3 + 1) * 128)
                    e.matmul(o3_[:, j3, :], Sc[:, 0, cs_], qa_[:, 0, :], start=(j3 == 0), stop=False)
                    e.matmul(o3_[:, j3, :], Sc[:, 1, cs_], qa_[:, 1, :], start=False, stop=False)
                    ins = e.matmul(o3_[:, j3, :], vx_[:, cs_], sq_[:], start=False, stop=True)
                return ins
            cx.op("pe", mmo, reads=[Sc, qa_, vx_, sq_], writes=[o3_])
            cx.op("act", lambda e: e.copy(xy_[:, :, (c % 8) * L:(c % 8 + 1) * L], o3_[:]), reads=[o3_], pwrites=[xy_])
            cx.op("dve", lambda e: e.scalar_tensor_tensor(Sf[:], Sf[:], a_bc[:, c:c + 1], U_[:, :, 0:384], ALU.mult, ALU.add),
                  reads=[Sf, a_bc, U_], writes=[Sf])
            cx.op("act", lambda e: e.copy(Sn[:], Sf[:]), reads=[Sf], writes=[Sn])
            if c % 8 == 7:
                gs = slice(g * 512, (g + 1) * 512)
                rb_, oz_, ssr_, yo_ = rb[g % 2], oz[g % 2], ssr[g % 2], yo[g % 2]
                cx.dma("sp", rb_[:], gsc[0:1, gs].partition_broadcast(128), rb_, reads=[gscv], writes=[rb_])
                cx.dma("sp", oz_[:], uT[1792:2304, gs].rearrange("(i p) t -> p i t", p=128), oz_, writes=[oz_])
                cx.op("dve", lambda e: e.tensor_tensor(t1[:], rb_[:], xy_[:, 2, :], ALU.mult), reads=[rb_, xy_], writes=[t1])
                cx.op("act", lambda e: e.activation(out=t1[:], in_=t1[:], func=AF.Abs), reads=[t1], writes=[t1])
                cx.op("dve", lambda e: e.tensor_scalar(t1[:], t1[:], 1.0, None, ALU.max), reads=[t1], writes=[t1])
                cx.op("dve", lambda e: e.reciprocal(t1[:], t1[:]), reads=[t1], writes=[t1])
                cx.op("dve", lambda e: e.tensor_tensor(t1[:], t1[:], rb_[:], ALU.mult), reads=[t1, rb_], writes=[t1])
                for j in range(2):
                    cx.op("dve", lambda e: e.tensor_tensor(hT[:, j, :], xy_[:, j, :], t1[:], ALU.mult), reads=[xy_, t1], writes=[hT] if j == 0 else [], pwrites=[hT] if j else [])
                cx.op("act", lambda e: e.activation(out=sqh[:], in_=hT[:], func=AF.Square), reads=[hT], writes=[sqh])

                def mmss(e):
                    e.matmul(ssp[:], ones[:], sqh[:, 0, :], start=True, stop=False)
                    return e.matmul(ssp[:], ones[:], sqh[:, 1, :], start=False, stop=True)
                cx.op("pe", mmss, reads=[ones, sqh], writes=[ssp])
                cx.op("dve", lambda e: e.tensor_copy(ssr_[:], ssp[0:1, :]), reads=[ssp], writes=[ssr_])
                cx.dma("pool", ssm[0:1, gs], ssr_[:], ssr_, reads=[ssr_])
                cx.op("act", lambda e: e.activation(out=sg[:], in_=oz_[:], func=AF.Sigmoid), reads=[oz_], writes=[sg])
                cx.op("pool", lambda e: e.tensor_tensor(tz[:], oz_[:, 2:4, :], sg[:, 2:4, :], ALU.mult), reads=[oz_, sg], writes=[tz])
                for j in range(2):
                    cx.op("dve", lambda e: e.scalar_tensor_tensor(hT[:, j, :], hT[:, j, :], mg[:, 2 * l + j:2 * l + j + 1], sg[:, j, :], ALU.mult, ALU.mult),
                          reads=[hT, mg, sg], writes=[hT])
                cx.op("dve", lambda e: e.tensor_tensor(yo_[:], hT[:], tz[:], ALU.mult), reads=[hT, tz], writes=[yo_])
                cx.dma("pool", yT[g, :, 2:4, :], yo_[:], yo_, reads=[yo_])


def emit_pass1(cx, xin, xout, sspD, op=None):
    NB = S // 512
    with cx.scope():
        ones = cx.sbuf("p_ones", [128, 128], BF16)
        cx.op("dve", lambda e: e.memset(ones[:], 1.0), writes=[ones])
        xb = [cx.sbuf("p_xb%d" % i, [128, 4, 512], F32, dma=True) for i in range(2)]
        sq = [cx.sbuf("p_sq%d" % i, [128, 4, 512], BF16) for i in range(2)]
        ssr = [cx.sbuf("p_ssr%d" % i, [1, 512], F32, dma=True) for i in range(2)]
        ssps = cx.psum("p_ssps", [128, 512], F32)
        if op is not None:
            wob = cx.sbuf("p_wob", [128, KT, 512], BF16)
            wst = [cx.sbuf("p_wst%d" % i, [128, 512], F32, dma=True) for i in range(4)]
            wobv = cx.virt("p_wobv")
            sel = cx.sbuf("p_sel", [8, 4 * 128], F32, dma=True)
            cx.dma("sp", sel[:], op["selD"][:, :], sel, writes=[sel])
            for k in range(KT):
                s_ = wst[k % 4]
                cx.dma("sp" if k % 2 == 0 else "act", s_[:], op["wo"][k * 128:(k + 1) * 128, :], s_, writes=[s_])
                if k % 2 == 0:
                    cx.op("dve", lambda e: e.tensor_copy(wob[:, k, :], s_[:]), reads=[s_], pwrites=[wobv])
                else:
                    cx.op("pool", lambda e: e.tensor_copy(wob[:, k, :], s_[:]), reads=[s_], pwrites=[wobv])
            yb = [cx.sbuf("p_yb%d" % i, [128, KT, 512], BF16) for i in range(2)]
            ybs = [rr(cx, "p_ybs%d_" % i, 4) for i in range(2)]
            ybv = [cx.virt("p_ybv%d" % i) for i in range(2)]
            ssmb = [cx.sbuf("p_ssmb%d" % i, [8, 512], F32, dma=True) for i in range(2)]
            rs = [cx.sbuf("p_rs%d" % i, [128, 512], F32) for i in range(2)]
            rsp = [cx.psum("p_rsp%d" % i, [128, 512], F32) for i in range(2)]
            ps = [cx.psum("p_ps%d" % i, [128, 512], F32) for i in range(3)]
            x2 = [cx.sbuf("p_x2%d" % i, [128, 4, 512], F32, dma=True) for i in range(2)]
        it = 0
        if op is not None:
            epsb = cx.sbuf("p_eps", [128, 1], F32)
            cx.op("dve", lambda e: e.memset(epsb[:], EPS), writes=[epsb])

            def prep_loads(tb):
                yb_, ys_, yv_, sm_ = yb[tb % 2], ybs[tb % 2], ybv[tb % 2], ssmb[tb % 2]
                for q in range(4):
                    for r in (2 * q, 2 * q + 1):
                        cx.dma("sp" if r % 2 == 0 else "act", yb_[:, r * 4:(r + 1) * 4, :], op["yTf"][r, tb], ys_[q],
                               writes=[ys_[q]] if r % 2 == 0 else [], pwrites=[ys_[q]] if r % 2 else [], reads=[yv_])
                cx.dma("sp", sm_[:], op["ssmf"][:, tb * 512:(tb + 1) * 512], sm_, writes=[sm_])

            def prep_head(tb, hd):
                yb_, ys_, yv_, sm_ = yb[tb % 2], ybs[tb % 2], ybv[tb % 2], ssmb[tb % 2]
                rp_, rs_ = rsp[hd % 2], rs[hd % 2]
                cx.op("pe", lambda e: e.matmul(rp_[:], sel[:, hd * 128:(hd + 1) * 128], sm_[:], start=True, stop=True), reads=[sel, sm_], writes=[rp_])
                cx.op("act", lambda e: e.activation(out=rs_[:], in_=rp_[:], func=AF.Ln, bias=epsb[:, 0:1], scale=1.0 / 512), reads=[rp_, epsb], writes=[rs_])
                cx.op("act", lambda e: e.activation(out=rs_[:], in_=rs_[:], func=AF.Exp, scale=-0.5), reads=[rs_], writes=[rs_])
                n_ = 0
                for r in (2 * hd, 2 * hd + 1):
                    for j in (2, 3):
                        kt = r * 4 + j
                        en = "pool" if n_ % 2 else "dve"
                        cx.op(en, lambda e: e.tensor_tensor(yb_[:, kt, :], yb_[:, kt, :], rs_[:], ALU.mult), reads=[rs_] + ys_, pwrites=[yv_])
                        n_ += 1

            prep_loads(0)
            for hd in range(4):
                prep_head(0, hd)
        for tb in range(NB):
            ts_ = slice(tb * 512, (tb + 1) * 512)
            xb_, sq_, ssr_ = xb[tb % 2], sq[tb % 2], ssr[tb % 2]
            cx.dma("sp", xb_[:], xin[:, ts_].rearrange("(j p) t -> p j t", p=128), xb_, writes=[xb_])
            if op is not None:
                yb_, ys_, yv_, x2_ = yb[tb % 2], ybs[tb % 2], ybv[tb % 2], x2[tb % 2]
                if tb + 1 < NB:
                    prep_loads(tb + 1)
                for dt in range(4):
                    p_ = ps[it % 3]

                    def mm(e):
                        for k in range(KT):
                            ins = e.matmul(p_[:], wob[:, k, dt * 128:(dt + 1) * 128], yb_[:, k, :], start=(k == 0), stop=(k == KT - 1))
                        return ins
                    cx.op("pe", mm, reads=[wobv, yv_] + ys_, writes=[p_])
                    cx.op("dve", lambda e: e.tensor_tensor(x2_[:, dt, :], p_[:], xb_[:, dt, :], ALU.add), reads=[p_, xb_], writes=[x2_] if dt == 0 else [], pwrites=[x2_] if dt else [])
                    cx.op("dve", lambda e: e.tensor_tensor(sq_[:, dt, :], x2_[:, dt, :], x2_[:, dt, :], ALU.mult), reads=[x2_], writes=[sq_] if dt == 0 else [], pwrites=[sq_] if dt else [])
                    if tb + 1 < NB:
                        prep_head(tb + 1, dt)
                    it += 1
                cx.dma("pool", xout[:, ts_].rearrange("(j p) t -> p j t", p=128), x2_[:], x2_, reads=[x2_])
            else:
                cx.op("act", lambda e: e.activation(out=sq_[:], in_=xb_[:], func=AF.Square), reads=[xb_], writes=[sq_])

            def mmss(e):
                for j in range(4):
                    ins = e.matmul(ssps[:], ones[:], sq_[:, j, :], start=(j == 0), stop=(j == 3))
                return ins
            cx.op("pe", mmss, reads=[ones, sq_], writes=[ssps])
            cx.op("dve", lambda e: e.tensor_copy(ssr_[:], ssps[0:1, :]), reads=[ssps], writes=[ssr_])
            cx.dma("pool", sspD[0:1, ts_], ssr_[:], ssr_, reads=[ssr_])


def emit_pass2(cx, xsrc, ssfD, gnD, lidx, dst, final):
    NB = S // 512
    with cx.scope():
        onesf = cx.sbuf("q_1f", [8, 128], F32)
        cx.op("dve", lambda e: e.memset(onesf[:], 1.0), writes=[onesf])
        epsq = cx.sbuf("q_eps", [128, 1], F32)
        cx.op("dve", lambda e: e.memset(epsq[:], EPS), writes=[epsq])
        g = cx.sbuf("q_g", [128, 12], F32, dma=True)
        cx.dma("sp", g[:], gnD[:, :], g, writes=[g])
        xb = [cx.sbuf("q_xb%d" % i, [128, 4, 512], F32, dma=True) for i in range(2)]
        ssb = [cx.sbuf("q_ssb%d" % i, [8, 512], F32, dma=True) for i in range(2)]
        tot = [cx.psum("q_tot%d" % i, [128, 512], F32) for i in range(2)]
        rstd = [cx.sbuf("q_rstd%d" % i, [128, 512], F32) for i in range(2)]
        ho = [cx.sbuf("q_ho%d" % i, [128, 4, 512], F32 if final else BF16, dma=True) for i in range(2)]
        for tb in range(NB):
            ts_ = slice(tb * 512, (tb + 1) * 512)
            xb_, sb_, t_, r_, h_ = xb[tb % 2], ssb[tb % 2], tot[tb % 2], rstd[tb % 2], ho[tb % 2]
            cx.dma("sp", xb_[:], xsrc[:, ts_].rearrange("(j p) t -> p j t", p=128), xb_, writes=[xb_])
            cx.dma("sp", sb_[:], ssfD[:, ts_], sb_, writes=[sb_])
            cx.op("pe", lambda e: e.matmul(t_[:], onesf[:], sb_[:], start=True, stop=True), reads=[onesf, sb_], writes=[t_])
            cx.op("act", lambda e: e.activation(out=r_[:], in_=t_[:], func=AF.Ln, bias=epsq[:, 0:1], scale=1.0 / D), reads=[t_, epsq], writes=[r_])
            cx.op("act", lambda e: e.activation(out=r_[:], in_=r_[:], func=AF.Exp, scale=-0.5), reads=[r_], writes=[r_])
            for j in range(4):
                cx.op("dve", lambda e: e.scalar_tensor_tensor(h_[:, j, :], xb_[:, j, :], g[:, lidx * 4 + j:lidx * 4 + j + 1], r_[:], ALU.mult, ALU.mult),
                      reads=[xb_, g, r_], writes=[h_] if j == 0 else [], pwrites=[h_] if j else [])
            cx.dma("pool", dst[:, ts_].rearrange("(j p) t -> p j t", p=128), h_[:], h_, reads=[h_])


def allgather(cx, src, dst):
    cx.barrier()
    if "cc" not in cx.sems:
        cx.sems["cc"] = cx.ges.enter_context(cx.nc.semaphore("cc"))
        cx.cnt["cc"] = 0
    cx.nc.gpsimd.collective_compute("AllGather", ALU.bypass, replica_groups=[list(range(NCORES))],
                                    ins=[src.opt()], outs=[dst.opt()]).then_inc(cx.sems["cc"])
    cx.cnt["cc"] += 1
    cx.barrier()


def build_fused():
    nc = bass.Bass("TRN2", target_bir_lowering=False)
    dt_ = nc.dram_tensor
    xT = dt_("xT", [512, S], F32, kind="ExternalInput").ap()
    gn = dt_("gn", [128, 12], F32, kind="ExternalInput").ap()
    w_in = dt_("w_in", [DEPTH, D, NWC], F32, kind="ExternalInput").ap()
    w_out = dt_("w_out", [DEPTH, D, 512], F32, kind="ExternalInput").ap()
    cwD = dt_("cw", [128, 32], F32, kind="ExternalInput").ap()
    bgD = dt_("bg", [128, 4], F32, kind="ExternalInput").ap()
    mgD = dt_("mg", [128, 4], F32, kind="ExternalInput").ap()
    biasD = dt_("bias", [128, 1536], F32, kind="ExternalInput").ap()
    identD = dt_("ident", [128, 128], F32, kind="ExternalInput").ap()
    triD = dt_("tri", [64, 64], F32, kind="ExternalInput").ap()
    selD = dt_("sel", [8, 512], F32, kind="ExternalInput").ap()
    outT = dt_("outT", [512, S], F32, kind="ExternalOutput").ap()
    x2a = dt_("x2a", [512, S], F32, kind="Internal").ap()
    x2b = dt_("x2b", [512, S], F32, kind="Internal").ap()
    ssp = dt_("ssp", [1, S], F32, kind="Internal").ap()
    ssf = dt_("ssf", [8, S], F32, kind="Internal").ap()
    hTc = dt_("hTc", [512, S], BF16, kind="Internal").ap()
    hTf = dt_("hTf", [D, S], BF16, kind="Internal").ap()
    uT = dt_("uT", [2304, S], BF16, kind="Internal").ap()
    uG = dt_("uG", [2, S], F32, kind="Internal").ap()
    gsc = dt_("gsc", [1, S], F32, kind="Internal").ap()
    yTc = dt_("yTc", [16, 128, 4, 512], BF16, kind="Internal").ap()
    yTf = dt_("yTf", [8, 16, 128, 4, 512], BF16, kind="Internal").ap()
    ssm = dt_("ssm", [1, S], F32, kind="Internal").ap()
    ssmf = dt_("ssmf", [8, S], F32, kind="Internal").ap()
    with ExitStack() as es:
        cx = Ctx(nc, es)
        emit_pass1(cx, xT, None, ssp, None)
        allgather(cx, ssp, ssf)
        emit_pass2(cx, xT, ssf, gn, 0, hTc, False)
        allgather(cx, hTc, hTf)
        xcur = xT
        for l in range(DEPTH):
            emit_inproj(cx, hTf, w_in[l], uT, uG)
            emit_attention(cx, uT, biasD, identD, yTc)
            emit_mlstm(cx, uT, uG, cwD, bgD, mgD, identD, triD, l, yTc, ssm, gsc)
            allgather(cx, yTc, yTf)
            allgather(cx, ssm, ssmf)
            xnew = x2a if l == 0 else x2b
            emit_pass1(cx, xcur, xnew, ssp, dict(yTf=yTf, ssmf=ssmf, wo=w_out[l], selD=selD))
            allgather(cx, ssp, ssf)
            last = (l == DEPTH - 1)
            emit_pass2(cx, xnew, ssf, gn, l + 1, outT if last else hTc, last)
            if not last:
                allgather(cx, hTc, hTf)
            xcur = xnew
        cx.barrier()
    return nc


_NC_CACHE = {}


def kernel(x, norm_g, w_in, b_gate, conv_w, mlstm_norm_g, w_out, rel_bias, final_g):
    x = np.asarray(x, np.float32)
    norm_g, w_in, b_gate, conv_w = (np.asarray(a, np.float32) for a in (norm_g, w_in, b_gate, conv_w))
    mlstm_norm_g, w_out, rel_bias, final_g = (np.asarray(a, np.float32) for a in (mlstm_norm_g, w_out, rel_bias, final_g))
    consts, pc = prep_common(norm_g, final_g, b_gate, conv_w, mlstm_norm_g, rel_bias)
    sel = np.zeros((8, 4, 128), np.float32)
    for hd in range(4):
        sel[2 * hd:2 * hd + 2, hd, :] = 1.0
    yo = ycol_order()
    xTfull = x[0].T
    in_maps = []
    for c in range(NCORES):
        cols = core_cols(c)
        rows = slice(512 * c, 512 * (c + 1))
        gn_c = np.ascontiguousarray(consts["gn"].reshape(128, 3, 32)[:, :, 4 * c:4 * c + 4]).reshape(128, 12)
        in_maps.append({
            "xT": np.ascontiguousarray(xTfull[rows]),
            "gn": gn_c,
            "w_in": np.ascontiguousarray(w_in[:, :, cols]),
            "w_out": np.ascontiguousarray(w_out[:, yo, :][:, :, rows]),
            "cw": pc[c]["cw"], "bg": pc[c]["bg"], "mg": pc[c]["mg"], "bias": pc[c]["bias"],
            "ident": consts["ident"], "tri": consts["tri"], "sel": sel.reshape(8, 512),
        })
    if "nc" not in _NC_CACHE:
        _NC_CACHE["nc"] = build_fused()
    res = run_bass_kernel_spmd(_NC_CACHE["nc"], in_maps, core_ids=list(range(NCORES)))
    outT = np.concatenate([res.results[c]["outT"] for c in range(NCORES)], axis=0)
    return np.ascontiguousarray(outT.T)[None].astype(np.float32)
```
